# Optimizing a Trainium2 kernel written in Bass

```python
import math
import jax, jax.numpy as jnp
from jax import lax
import numpy as np

D_MODEL = 2048
BATCH = 8
SEQ = 2048
DEPTH = 1

HYENA_WIDTH = 1024
CONV_WIDTH = 3
FILTER_EMB_DIM = 33
FILTER_HIDDEN = 64
DECAY_TARGET = 1e-2
FAST_DECAY_PCT = 0.3
SLOW_DECAY_PCT = 1.5
DECAY_SHIFT = 0.05

N_HEADS = 8
QK_NOPE_DIM = 128
QK_ROPE_DIM = 64
V_HEAD_DIM = 128
Q_LORA_RANK = 512
KV_LORA_RANK = 256
ROPE_THETA = 10000.0
Q_BLOCK = 128

ATTN_WIDTH = N_HEADS * V_HEAD_DIM
MIX_WIDTH = HYENA_WIDTH + ATTN_WIDTH
IN_PROJ_WIDTH = 3 * HYENA_WIDTH + Q_LORA_RANK + KV_LORA_RANK + QK_ROPE_DIM

D_FF = 5632
NORM_EPS = 1e-6

kernel_name = "hybrid_hyena_mla_convffn_sandwich"


def _rmsnorm(x, gain):
    xf = x.astype(jnp.float32)
    y = xf * lax.rsqrt(jnp.mean(xf * xf, axis=-1, keepdims=True) + NORM_EPS)
    return (y * gain.astype(jnp.float32)).astype(x.dtype)


def _dwconv_centred(x, w, b):
    k = w.shape[0]
    half = k // 2
    s = x.shape[1]
    xp = jnp.pad(x, ((0, 0), (half, half), (0, 0)))
    return sum(xp[:, j:j + s] * w[j] for j in range(k)) + b


def _position_features(L):
    t = jnp.linspace(0.0, 1.0, L, dtype=jnp.float32)[:, None]
    bands = (FILTER_EMB_DIM - 1) // 2
    w = 2.0 * math.pi * jnp.arange(L, dtype=jnp.float32) / L
    f = jnp.linspace(1e-4, bands - 1, bands, dtype=jnp.float32)
    ang = w[:, None] * f[None, :]
    return t, jnp.concatenate([t, jnp.cos(ang), -jnp.sin(ang)], axis=-1)


def _implicit_filters(t, z, w1, b1, fr1, w2, b2, fr2, w3):
    f32 = jnp.float32
    L = t.shape[0]
    h = jnp.sin(fr1.astype(f32) * (z @ w1.astype(f32) + b1.astype(f32)))
    h = jnp.sin(fr2.astype(f32) * (h @ w2.astype(f32) + b2.astype(f32)))
    h = h @ w3.astype(f32)
    max_decay = math.log(DECAY_TARGET) / FAST_DECAY_PCT
    min_decay = math.log(DECAY_TARGET) / SLOW_DECAY_PCT
    deltas = jnp.abs(jnp.linspace(min_decay, max_decay, HYENA_WIDTH, dtype=f32))
    window = jnp.exp(-t * deltas[None, :]) + DECAY_SHIFT
    h = h.reshape(L, 2, HYENA_WIDTH) * window[:, None, :]
    h_fwd, h_bwd = h[:, 0], h[:, 1]
    return jnp.concatenate([h_fwd, jnp.zeros((1, HYENA_WIDTH), f32), h_bwd[1:][::-1]], axis=0)


def _bidir_long_conv(u, k2, bias):
    L = u.shape[1]
    n = 2 * L
    uf = jnp.fft.rfft(u.astype(jnp.float32), n=n, axis=1)
    kf = jnp.fft.rfft(k2, n=n, axis=0)
    y = jnp.fft.irfft(uf * kf[None], n=n, axis=1)[:, :L]
    return (y + u.astype(jnp.float32) * bias.astype(jnp.float32)).astype(u.dtype)


def _rope(x, cos, sin):
    x1, x2 = jnp.split(x, 2, axis=-1)
    return x * cos + jnp.concatenate([-x2, x1], axis=-1) * sin


def _mla(c_q, c_kv, k_pe, q_norm_gain, w_uq, kv_norm_gain, w_ukv, cos, sin):
    B, S, _ = c_q.shape
    dqk = QK_NOPE_DIM + QK_ROPE_DIM
    q = (_rmsnorm(c_q, q_norm_gain) @ w_uq).reshape(B, S, N_HEADS, dqk)
    q_nope, q_pe = q[..., :QK_NOPE_DIM], q[..., QK_NOPE_DIM:]
    kv = (_rmsnorm(c_kv, kv_norm_gain) @ w_ukv).reshape(B, S, N_HEADS, QK_NOPE_DIM + V_HEAD_DIM)
    k_nope, v = kv[..., :QK_NOPE_DIM], kv[..., QK_NOPE_DIM:]
    q_pe = _rope(q_pe, cos[:, None, :], sin[:, None, :])
    k_pe = _rope(k_pe, cos, sin)
    q = jnp.concatenate([q_nope, q_pe], axis=-1)
    k = jnp.concatenate([k_nope, jnp.broadcast_to(k_pe[:, :, None, :], (B, S, N_HEADS, QK_ROPE_DIM))], axis=-1)
    scale = dqk ** -0.5
    nb = S // Q_BLOCK
    qb = q.reshape(B, nb, Q_BLOCK, N_HEADS, dqk).transpose(1, 0, 2, 3, 4)

    def attend(q_blk):
        s = jnp.einsum('bqhd,bkhd->bhqk', q_blk, k).astype(jnp.float32) * scale
        p = jax.nn.softmax(s, axis=-1).astype(v.dtype)
        return jnp.einsum('bhqk,bkhd->bqhd', p, v)

    o = lax.map(attend, qb)
    return o.transpose(1, 0, 2, 3, 4).reshape(B, S, ATTN_WIDTH)


def setup_inputs(seed: int = 0) -> dict:
    key = jax.random.key(seed)
    ks = jax.random.split(key, 32)
    f32 = jnp.float32

    def nrm(k, shape, scale):
        return jax.random.normal(k, shape, f32) * scale

    def gain(k, n):
        return 1.0 + 0.05 * jax.random.normal(k, (DEPTH, n), f32)

    return {
        "x": nrm(ks[0], (BATCH, SEQ, D_MODEL), 1.0),
        "pre_mix_gain": gain(ks[1], D_MODEL),
        "w_in": nrm(ks[2], (DEPTH, D_MODEL, IN_PROJ_WIDTH), D_MODEL ** -0.5),
        "hyena_conv_w": nrm(ks[3], (DEPTH, CONV_WIDTH, 3 * HYENA_WIDTH), CONV_WIDTH ** -0.5),
        "hyena_conv_b": nrm(ks[4], (DEPTH, 3 * HYENA_WIDTH), 0.02),
        "filt_w1": nrm(ks[5], (DEPTH, FILTER_EMB_DIM, FILTER_HIDDEN), FILTER_EMB_DIM ** -0.5),
        "filt_b1": nrm(ks[6], (DEPTH, FILTER_HIDDEN), 0.02),
        "filt_freq1": 1.0 + 0.1 * jax.random.normal(ks[7], (DEPTH, FILTER_HIDDEN), f32),
        "filt_w2": nrm(ks[8], (DEPTH, FILTER_HIDDEN, FILTER_HIDDEN), FILTER_HIDDEN ** -0.5),
        "filt_b2": nrm(ks[9], (DEPTH, FILTER_HIDDEN), 0.02),
        "filt_freq2": 1.0 + 0.1 * jax.random.normal(ks[10], (DEPTH, FILTER_HIDDEN), f32),
        "filt_w3": nrm(ks[11], (DEPTH, FILTER_HIDDEN, 2 * HYENA_WIDTH), FILTER_HIDDEN ** -0.5),
        "hyena_bias": nrm(ks[12], (DEPTH, HYENA_WIDTH), 0.5),
        "q_norm_gain": gain(ks[13], Q_LORA_RANK),
        "w_uq": nrm(ks[14], (DEPTH, Q_LORA_RANK, N_HEADS * (QK_NOPE_DIM + QK_ROPE_DIM)), Q_LORA_RANK ** -0.5),
        "kv_norm_gain": gain(ks[15], KV_LORA_RANK),
        "w_ukv": nrm(ks[16], (DEPTH, KV_LORA_RANK, N_HEADS * (QK_NOPE_DIM + V_HEAD_DIM)), KV_LORA_RANK ** -0.5),
        "hyena_out_gain": gain(ks[17], HYENA_WIDTH),
        "attn_out_gain": gain(ks[18], ATTN_WIDTH),
        "w_out": nrm(ks[19], (DEPTH, MIX_WIDTH, D_MODEL), MIX_WIDTH ** -0.5),
        "post_mix_gain": gain(ks[20], D_MODEL),
        "pre_ffn_gain": gain(ks[21], D_MODEL),
        "w_up": nrm(ks[22], (DEPTH, D_MODEL, 2 * D_FF), D_MODEL ** -0.5),
        "ffn_conv_w": nrm(ks[23], (DEPTH, CONV_WIDTH, D_FF), CONV_WIDTH ** -0.5),
        "ffn_conv_b": nrm(ks[24], (DEPTH, D_FF), 0.02),
        "w_down": nrm(ks[25], (DEPTH, D_FF, D_MODEL), D_FF ** -0.5),
        "post_ffn_gain": gain(ks[26], D_MODEL),
    }


def reference(x, pre_mix_gain, w_in, hyena_conv_w, hyena_conv_b, filt_w1, filt_b1, filt_freq1,
              filt_w2, filt_b2, filt_freq2, filt_w3, hyena_bias, q_norm_gain, w_uq, kv_norm_gain,
              w_ukv, hyena_out_gain, attn_out_gain, w_out, post_mix_gain, pre_ffn_gain, w_up,
              ffn_conv_w, ffn_conv_b, w_down, post_ffn_gain):
    B, S, _ = x.shape
    pos = jnp.arange(S, dtype=jnp.float32)
    inv_freq = 1.0 / (ROPE_THETA ** (jnp.arange(0, QK_ROPE_DIM, 2, dtype=jnp.float32) / QK_ROPE_DIM))
    ang = pos[:, None] * inv_freq[None, :]
    ang = jnp.concatenate([ang, ang], axis=-1)
    cos = jnp.cos(ang).astype(x.dtype)
    sin = jnp.sin(ang).astype(x.dtype)
    t, z = _position_features(S)

    h = x
    for l in range(DEPTH):
        xn = _rmsnorm(h, pre_mix_gain[l])
        proj = xn @ w_in[l]
        o1 = 3 * HYENA_WIDTH
        o2 = o1 + Q_LORA_RANK
        o3 = o2 + KV_LORA_RANK
        xh, c_q, c_kv, k_pe = proj[..., :o1], proj[..., o1:o2], proj[..., o2:o3], proj[..., o3:]

        xh = _dwconv_centred(xh, hyena_conv_w[l], hyena_conv_b[l])
        x0 = xh[..., :HYENA_WIDTH]
        x1 = xh[..., HYENA_WIDTH:2 * HYENA_WIDTH]
        v = xh[..., 2 * HYENA_WIDTH:]
        k2 = _implicit_filters(t, z, filt_w1[l], filt_b1[l], filt_freq1[l], filt_w2[l], filt_b2[l],
                               filt_freq2[l], filt_w3[l])
        y_hyena = x0 * _bidir_long_conv(x1 * v, k2, hyena_bias[l])

        y_attn = _mla(c_q, c_kv, k_pe, q_norm_gain[l], w_uq[l], kv_norm_gain[l], w_ukv[l], cos, sin)

        mixed = jnp.concatenate([_rmsnorm(y_hyena, hyena_out_gain[l]),
                                 _rmsnorm(y_attn, attn_out_gain[l])], axis=-1) @ w_out[l]
        h = h + _rmsnorm(mixed, post_mix_gain[l])

        hn = _rmsnorm(h, pre_ffn_gain[l])
        gu = hn @ w_up[l]
        g = _dwconv_centred(gu[..., :D_FF], ffn_conv_w[l], ffn_conv_b[l])
        u = gu[..., D_FF:]
        f = (jax.nn.gelu(g, approximate=True) * u) @ w_down[l]
        h = h + _rmsnorm(f, post_ffn_gain[l])
    return h
```

```python
import math
import os
import types
from contextlib import ExitStack

import numpy as np
import ml_dtypes

import concourse.bass as bass
import concourse.mybir as mybir
from concourse.bass_utils import run_bass_kernel_spmd

F32 = mybir.dt.float32
BF16 = mybir.dt.bfloat16
AF = mybir.ActivationFunctionType
ALU = mybir.AluOpType

D = 2048
S = 2048
HW = 1024
NH = 8
DFF = 5632
NFF = DFF // 128
EPS = 1e-6
INW = 3904
PI = math.pi
NCORES = 8

POOL_BYTES = 206 * 1024

DEBUG = os.environ.get("MK_DEBUG", "")


class Buf:
    def __init__(self, name, off=None, nbytes=0):
        self.name = name
        self.off = off
        self.nbytes = nbytes
        self.w = {}
        self.r = {}
        self.dkey = None
        self.excl = False


def _snap(fn):
    if fn.__closure__ is None:
        return fn
    cells = []
    for c in fn.__closure__:
        try:
            cells.append(types.CellType(c.cell_contents))
        except ValueError:
            cells.append(c)
    g = types.FunctionType(fn.__code__, fn.__globals__, fn.__name__, fn.__defaults__, tuple(cells))
    g.__kwdefaults__ = fn.__kwdefaults__
    return g


class Prog:
    ENGS = ("pe", "act", "dve", "pool", "sp")

    def __init__(self, nc, es, pool_ap):
        self.nc = nc
        self.es = es
        self.streams = {e: [] for e in self.ENGS}
        self.semh = {}
        self.cnt = {}
        self.seen = {e: {} for e in self.ENGS}
        self.pool_ap = pool_ap
        self.free_list = [(0, POOL_BYTES)]
        self.ghosts = []
        self.nsem = 0
        self.peak = 0
        self.used = 0
        self.dead = False
        self.dumps = []

    def sem(self, key):
        if key not in self.semh:
            self.semh[key] = self.es.enter_context(self.nc.semaphore("s%d" % self.nsem))
            self.nsem += 1
            self.cnt[key] = 0
        return self.semh[key]

    def alloc(self, name, nbytes):
        req = nbytes
        nbytes = (nbytes + 63) // 64 * 64
        if self.dead:
            b = Buf(name, 0, nbytes)
            b.req = req
            return b
        for i, (s, e) in enumerate(self.free_list):
            if e - s >= nbytes:
                b = Buf(name, s, nbytes)
                b.req = req
                if e - s == nbytes:
                    self.free_list.pop(i)
                else:
                    self.free_list[i] = (s + nbytes, e)
                keep = []
                for (gs, ge, ev) in self.ghosts:
                    if gs < s + nbytes and ge > s:
                        for k, v in ev.items():
                            b.r[k] = max(b.r.get(k, 0), v)
                        if gs >= s and ge <= s + nbytes:
                            continue
                    keep.append((gs, ge, ev))
                self.ghosts = keep
                self.used += nbytes
                self.peak = max(self.peak, self.used)
                return b
        raise RuntimeError("SBUF pool exhausted allocating %s (%d B), used %d" % (name, nbytes, self.used))

    def free(self, b):
        if self.dead:
            return
        ev = dict(b.r)
        for k, v in b.w.items():
            ev[k] = max(ev.get(k, 0), v)
        self.ghosts.append((b.off, b.off + b.nbytes, ev))
        self.used -= b.nbytes
        fl = self.free_list + [(b.off, b.off + b.nbytes)]
        fl.sort()
        merged = []
        for s, e in fl:
            if merged and merged[-1][1] == s:
                merged[-1] = (merged[-1][0], e)
            else:
                merged.append((s, e))
        self.free_list = merged
        b.off = None

    def view(self, b, dtype=BF16, pattern=None, **kw):
        ap = self.pool_ap[:, b.off // 2:(b.off + b.req) // 2]
        if dtype != BF16:
            ap = ap.bitcast(dtype)
        if pattern is not None:
            ap = ap.rearrange(pattern, **kw)
        return ap

    def _wait(self, eng, deps, raw=None):
        for k, v in deps.items():
            if k == eng and eng == "pe":
                continue
            if k == eng and raw is not None and raw.get(k, 0) < v:
                if raw.get(k, 0) <= self.seen[eng].get(k, 0):
                    continue
                v = raw[k]
            if self.seen[eng].get(k, 0) >= v:
                continue
            self.seen[eng][k] = v
            h = self.sem(k)
            self.streams[eng].append(lambda e, h=h, v=v: e.wait_ge(h, v))

    def _collect(self, reads, writes, parts, eng=None):
        deps = {}

        def add(d):
            for k, v in d.items():
                if v > deps.get(k, 0):
                    deps[k] = v

        for b in reads:
            add(b.w)
            if b.excl:
                add({k: v for k, v in b.r.items() if k != eng})
        self._raw = dict(deps)
        for b in writes:
            add(b.w)
            add(b.r)
        for b in parts:
            add(b.r)
        return deps

    def _commit(self, key, val, reads, writes, parts):
        for b in reads:
            b.r[key] = max(b.r.get(key, 0), val)
        for b in writes:
            b.w = {key: val}
            b.r = {}
        for b in parts:
            b.w[key] = max(b.w.get(key, 0), val)

    def op(self, eng, fn, reads=(), writes=(), parts=(), tick=True):
        if self.dead:
            return
        fn = _snap(fn)
        deps = self._collect(reads, writes, parts, eng)
        self._wait(eng, deps, self._raw)
        h = self.sem(eng)
        if tick:
            self.cnt[eng] += 1
            val = self.cnt[eng]
            self.streams[eng].append(lambda e, fn=fn, h=h: fn(e).then_inc(h, 1))
        else:
            val = self.cnt[eng] + 1
            self.streams[eng].append(lambda e, fn=fn: fn(e))
        self._commit(eng, val, reads, writes, parts)

    def dma(self, q, out, in_, reads=(), writes=(), parts=(), sembuf=None):
        if self.dead:
            return
        deps = self._collect(reads, writes, parts)
        self._wait(q, deps)
        if sembuf.dkey is None:
            sembuf.dkey = ("d", sembuf.name, id(sembuf))
        key = sembuf.dkey
        h = self.sem(key)
        self.cnt[key] += 16
        val = self.cnt[key]
        self.streams[q].append(lambda e, out=out, in_=in_, h=h: e.dma_start(out=out, in_=in_).then_inc(h, 16))
        self._commit(key, val, reads, writes, parts)

    def prewait(self, eng, reads=()):
        if self.dead:
            return
        deps = {}
        for b in reads:
            for k, v in b.w.items():
                deps[k] = max(deps.get(k, 0), v)
        self._wait(eng, deps)

    def wait_all(self, eng, bufs):
        deps = {}
        for b in bufs:
            for d in (b.w, b.r):
                for k, v in d.items():
                    deps[k] = max(deps.get(k, 0), v)
        self._wait(eng, deps)


_CONST_CACHE = {}


def host_constants():
    if _CONST_CACHE:
        return _CONST_CACHE
    bf = ml_dtypes.bfloat16
    L = S
    n = 2 * L
    l = np.arange(L, dtype=np.int64)
    f = np.arange(L, dtype=np.int64)
    m = ((2 * f[None, :] + 1) * l[:, None]) % (2 * n)
    ang = (2.0 * np.pi / (2 * n)) * m.astype(np.float64)
    Cm = np.cos(ang)
    Sm = np.sin(ang)
    def fwd_layout(M):
        return M.reshape(16, 128, 16, 128).transpose(2, 1, 0, 3)
    csf = np.stack([fwd_layout(Cm), fwd_layout(Sm)], axis=2)
    _CONST_CACHE["dft_fwd"] = np.ascontiguousarray(csf.reshape(16, 128, 2 * 16 * 128)).astype(bf)
    def inv_layout(M):
        Mt = (2.0 / n) * M
        return Mt.reshape(8, 256, 16, 128).transpose(0, 3, 2, 1)
    csi = np.stack([inv_layout(Cm), inv_layout(Sm)], axis=2)
    _CONST_CACHE["dft_inv"] = np.ascontiguousarray(csi.reshape(8, 128, 2 * 16 * 256)).astype(bf)
    t = np.linspace(0.0, 1.0, L, dtype=np.float32)
    max_decay = math.log(1e-2) / 0.3
    min_decay = math.log(1e-2) / 1.5
    deltas = np.abs(np.linspace(min_decay, max_decay, HW, dtype=np.float32))
    win = np.exp(-t[:, None].astype(np.float64) * deltas[None, :].astype(np.float64)) + 0.05
    _CONST_CACHE["window"] = np.ascontiguousarray(win.astype(np.float32))
    bands = 16
    wv = (np.float32(2.0 * math.pi) * np.arange(L, dtype=np.float32) / np.float32(L)).astype(np.float32)
    fr = np.linspace(1e-4, bands - 1, bands, dtype=np.float32)
    a2 = (wv[:, None] * fr[None, :]).astype(np.float32).astype(np.float64)
    z = np.concatenate([t[:, None].astype(np.float64), np.cos(a2), -np.sin(a2)], axis=1)
    _CONST_CACHE["zT"] = np.ascontiguousarray(z.T.astype(np.float32))
    pos = np.arange(S, dtype=np.float32)
    inv_freq = (1.0 / (np.float32(10000.0) ** (np.arange(0, 64, 2, dtype=np.float32) / np.float32(64)))).astype(np.float32)
    ra = (pos[:, None] * inv_freq[None, :]).astype(np.float32)
    ra = np.concatenate([ra, ra], axis=1).astype(np.float64)
    cos = np.cos(ra)
    sin = np.sin(ra)
    sgn = np.concatenate([-np.ones(32), np.ones(32)])[None, :]
    sin_s = sin * sgn
    tok = np.stack([cos, sin_s], axis=1)
    tok = tok.reshape(16, 128, 2, 64).transpose(1, 0, 2, 3)
    _CONST_CACHE["rope_tok"] = np.ascontiguousarray(tok.reshape(128, 16 * 2 * 64)).astype(np.float32)
    scale = (128 + 64) ** -0.5
    ft = np.stack([cos.T * scale, sin_s.T * scale], axis=1)
    _CONST_CACHE["rope_feat"] = np.ascontiguousarray(ft.reshape(64, 2 * S)).astype(bf)
    _CONST_CACHE["ident"] = np.eye(128, dtype=np.float32).astype(bf)
    return _CONST_CACHE


def build_program(stop_after=None):
    nc = bass.Bass("TRN2", target_bir_lowering=False)

    def din(name, shape, dt=F32):
        return nc.dram_tensor(name, list(shape), dt, kind="ExternalInput").ap()

    x = din("x", [S, D])
    w_in = din("w_in", [D, INW])
    w_uq = din("w_uq", [512, 1536])
    w_ukv = din("w_ukv", [256, 2048])
    w_out = din("w_out", [D, D])
    w_up = din("w_up", [D, 2 * DFF])
    w_down = din("w_down", [DFF, D])
    filt_w1 = din("filt_w1", [33, 64])
    filt_w2 = din("filt_w2", [64, 64])
    filt_w3 = din("filt_w3", [64, 2048])
    pk_names = [
        ("hconv_w", 72), ("hconv_b", 24), ("hgain", 8), ("again", 8), ("fconv_w", 132), ("fconv_b", 44),
        ("fb1", 1), ("ffr1", 1), ("fb2", 1), ("ffr2", 1),
    ]
    pk_off = {}
    o = 0
    for nm, sz in pk_names:
        pk_off[nm] = (o, sz)
        o += sz
    NPK = o
    pack = din("pack", [128, NPK])
    gains_d = {nm: din(nm, [128, sz]) for nm, sz in (("g_pre", 2048), ("g_q", 512), ("g_kv", 256), ("g_post", 2048), ("g_ffn", 2048), ("g_pffn", 2048))}
    hbias = din("hbias", [1, HW])
    dft_fwd = din("dft_fwd", [16, 128, 2 * 16 * 128], BF16)
    dft_inv = din("dft_inv", [8, 128, 2 * 16 * 256], BF16)
    window = din("window", [S, HW])
    zT = din("zT", [33, S])
    rope_tok = din("rope_tok", [128, 16 * 2 * 64])
    rope_feat = din("rope_feat", [64, 2 * S], BF16)
    ident_d = din("ident", [128, 128], BF16)

    y = nc.dram_tensor("y", [S, D], F32, kind="ExternalOutput").ap()
    hn_scr = nc.dram_tensor("hn_scr", [128, 16, S], BF16, kind="Internal").ap()
    dbg = {}
    if DEBUG:
        for nm, shp, dt in [("d_xnT", [128, 16 * S], BF16), ("d_x0T", [128, 8 * S], BF16), ("d_utok", [128, 2 * 16 * 512], BF16),
                            ("d_cq", [128, 4 * S], BF16), ("d_ckv", [128, 2 * S], BF16), ("d_kpe", [128, S], BF16),
                            ("d_yh", [128, 8 * S], BF16), ("d_ya", [128, 8 * S], BF16), ("d_A", [128, 16 * 512], BF16)]:
            dbg[nm] = nc.dram_tensor(nm, shp, dt, kind="ExternalOutput").ap()

    with ExitStack() as es:
        pool_t = es.enter_context(nc.sbuf_tensor("pool", [128, POOL_BYTES // 2], BF16))
        psum_t = es.enter_context(nc.psum_tensor("psum", [128, 4096], F32))
        P = Prog(nc, es, pool_t[:, :])
        ps_f = psum_t[:, :]
        ps_b = psum_t[:, :].bitcast(BF16)
        banks = [Buf("bank%d" % i) for i in range(8)]
        for b_ in banks:
            b_.excl = True

        def pf(b, a=0, n=512, rows=128):
            return ps_f[0:rows, b * 512 + a: b * 512 + a + n]

        def pb(b, a=0, n=1024, rows=128):
            return ps_b[0:rows, b * 1024 + a: b * 1024 + a + n]

        cbuf = P.alloc("pack", NPK * 4)
        packv = P.view(cbuf, F32)

        def pkc(nm, a=0, n=None):
            o_, sz = pk_off[nm]
            if n is None:
                n = sz - a
            return packv[:, o_ + a:o_ + a + n]

        P.dma("sp", packv, pack[:, :], writes=[cbuf], sembuf=cbuf)

        def load_gain(nm):
            sz = gains_d[nm].shape[1]
            gb = P.alloc(nm, sz * 4)
            gv = P.view(gb, F32)
            P.dma("sp", gv, gains_d[nm][:, :], writes=[gb], sembuf=gb)
            return gb, gv
        identb = P.alloc("ident", 128 * 2)
        ident = P.view(identb)
        P.dma("sp", ident, ident_d[:, :], writes=[identb], sembuf=identb)
        onesb = P.alloc("ones", 128 * 2)
        ones = P.view(onesb)
        P.op("dve", lambda e: e.memset(ones, 1.0), writes=[onesb])
        smallb = P.alloc("small", 64 * 4)
        small = P.view(smallb, F32)
        junkb = P.alloc("junk", 2048 * 2)
        junk = P.view(junkb)

        sm_idx = [0]

        def rstd_from_ss(ss_ap, inv_n, ss_buf):
            i = sm_idx[0] % 60
            sm_idx[0] += 1
            col = small[:, i:i + 1]
            P.op("act", lambda e: e.activation(col, ss_ap, AF.Sqrt, bias=pk_eps, scale=inv_n), reads=[ss_buf, epsb], parts=[smallb])
            P.op("dve", lambda e: e.reciprocal(col, col), reads=[smallb], parts=[smallb])
            return col

        epsb = P.alloc("eps", 4 * 4)
        pk_eps = P.view(epsb, F32)[:, 0:1]
        P.op("dve", lambda e: e.memset(pk_eps, EPS), writes=[epsb])
        ssb = P.alloc("ss", 64 * 4)
        ssv = P.view(ssb, F32)
        ss_i = [0]

        def ss_col():
            i = ss_i[0] % 64
            ss_i[0] += 1
            return ssv[:, i:i + 1]

        gpre_b, gpre = load_gain("g_pre")
        xnTb = P.alloc("xnT", 16 * S * 2)
        xnT = P.view(xnTb, BF16, "p (c t) -> p c t", t=S)
        xts = [P.alloc("xt%d" % i, D * 4) for i in range(4)]
        xss = [P.alloc("xs%d" % i, D * 2) for i in range(3)]
        def p1_a(i):
            xt_b = xts[i % 4]
            xs_b = xss[i % 3]
            xt = P.view(xt_b, F32)
            xs = P.view(xs_b)
            P.dma("sp", xt, x[i * 128:(i + 1) * 128, :], writes=[xt_b], sembuf=xt_b)
            ssc = ss_col()
            P.op("act", lambda e, xt=xt, ssc=ssc: e.activation(junk, xt, AF.Square, accum_out=ssc), reads=[xt_b], parts=[ssb])
            rs = rstd_from_ss(ssc, 1.0 / D, ssb)
            P.op("dve", lambda e, xt=xt, xs=xs, rs=rs: e.scalar_tensor_tensor(xs, xt, rs, gpre, op0=ALU.mult, op1=ALU.mult),
                 reads=[xt_b, smallb, gpre_b], writes=[xs_b])

        def p1_b(i):
            xs_b = xss[i % 3]
            xs = P.view(xs_b)
            for hb in range(2):
                bk = (2 * i + hb) % 4
                for c in range(8):
                    cc = hb * 8 + c
                    P.op("pe", lambda e, bk=bk, c=c, cc=cc, xs=xs: e.transpose(pb(bk, c * 128, 128), xs[:, cc * 128:(cc + 1) * 128], ident),
                         reads=[xs_b, identb], parts=[banks[bk]] if c else (), writes=[banks[bk]] if c == 0 else (), tick=(c == 7))
                eng = "act" if hb == 0 else "dve"
                dst = xnT[:, hb * 8:(hb + 1) * 8, i * 128:(i + 1) * 128]
                src = pb(bk).rearrange("p (c t) -> p c t", t=128)
                if eng == "act":
                    P.op("act", lambda e, dst=dst, src=src: e.copy(dst, src), reads=[banks[bk]], parts=[xnTb])
                else:
                    P.op("dve", lambda e, dst=dst, src=src: e.tensor_copy(dst, src), reads=[banks[bk]], parts=[xnTb])

        p1_a(0)
        for i in range(16):
            if i + 1 < 16:
                p1_a(i + 1)
            p1_b(i)
        for b in xts + xss + [gpre_b]:
            P.free(b)

        def dump(name, buf, ap2d, col0=0):
            if name in dbg:
                n = ap2d.shape[1]
                for a in range(0, n, 4096):
                    b_ = min(n, a + 4096)
                    P.dma("sp", dbg[name][:, col0 + a:col0 + b_], ap2d[:, a:b_], reads=[buf], sembuf=buf)
                if not P.dead:
                    P.dumps.append(buf)

        def checkpoint(k):
            if stop_after is not None and k >= stop_after and not P.dead:
                P.wait_all("sp", P.dumps)
                P.dead = True

        dump("d_xnT", xnTb, P.view(xnTb))
        checkpoint(1)

        x0Tb = P.alloc("x0T", 8 * S * 2)
        x0T = P.view(x0Tb, BF16, "p (c t) -> p c t", t=S)
        utokb = [P.alloc("utok%d" % h, 16 * 512 * 2) for h in range(2)]
        utok = [P.view(b, BF16, "p (s c) -> p s c", c=512) for b in utokb]
        wsl = [P.alloc("wsl%d" % i, 16 * 128 * 2) for i in range(3)]
        cx1b = P.alloc("cx1", S * 4)
        cob = P.alloc("co", S * 4)
        ubb = P.alloc("ub", S * 2)
        cx1 = P.view(cx1b, F32)
        co = P.view(cob, F32)
        ub = P.view(ubb)
        widx = [0]

        def hy_chunk(col0, setk, part_idx, j):
            wb_ = wsl[widx[0] % 3]
            widx[0] += 1
            wv_ = P.view(wb_, BF16, "p (c n) -> p c n", n=128)
            P.dma("pool", wv_, w_in[:, col0:col0 + 128].rearrange("(c p) n -> p c n", p=128), writes=[wb_], sembuf=wb_)
            for tg in range(4):
                bk = setk * 4 + tg
                for c in range(16):
                    P.op("pe", lambda e, bk=bk, c=c, tg=tg, wv_=wv_: e.matmul(pf(bk), lhsT=wv_[:, c, :], rhs=xnT[:, c, tg * 512:(tg + 1) * 512],
                                                                      start=(c == 0), stop=(c == 15)),
                         reads=[wb_, xnTb], writes=[banks[bk]] if c == 0 else (), parts=[banks[bk]] if c else (), tick=(c == 15))

        def conv_chunk(setk, chunk24, dst, dst_buf):
            raw = ps_f[:, setk * 2048:(setk + 1) * 2048]
            bset = banks[setk * 4:(setk + 1) * 4]
            w0 = pkc("hconv_w", 0 * 24 + chunk24, 1)
            w1 = pkc("hconv_w", 1 * 24 + chunk24, 1)
            w2 = pkc("hconv_w", 2 * 24 + chunk24, 1)
            bb = pkc("hconv_b", chunk24, 1)
            P.op("act", lambda e: e.activation(dst, raw, AF.Identity, bias=bb, scale=w1), reads=bset + [cbuf], writes=[dst_buf])
            P.op("dve", lambda e: e.scalar_tensor_tensor(dst[:, 1:S], raw[:, 0:S - 1], w0, dst[:, 1:S], op0=ALU.mult, op1=ALU.add),
                 reads=bset + [cbuf], writes=[dst_buf])
            P.op("dve", lambda e: e.scalar_tensor_tensor(dst[:, 0:S - 1], raw[:, 1:S], w2, dst[:, 0:S - 1], op0=ALU.mult, op1=ALU.add),
                 reads=bset + [cbuf], writes=[dst_buf])

        for j in range(8):
            sa, sb_ = (0, 1) if j % 2 == 0 else (1, 0)
            hy_chunk(HW + j * 128, sa, 1, j)
            hy_chunk(2 * HW + j * 128, sb_, 2, j)
            conv_chunk(sa, 8 + j, cx1, cx1b)
            hy_chunk(j * 128, sa, 0, j)
            conv_chunk(sb_, 16 + j, co, cob)
            P.op("dve", lambda e: e.tensor_tensor(ub, cx1, co, op=ALU.mult), reads=[cx1b, cob], writes=[ubb])
            hf = j // 4
            jj = j % 4
            for hb in range(2):
                bk = sb_ * 4 + hb
                for c in range(8):
                    sc = hb * 8 + c
                    P.op("pe", lambda e, bk=bk, c=c, sc=sc: e.transpose(pb(bk, c * 128, 128), ub[:, sc * 128:(sc + 1) * 128], ident),
                         reads=[ubb, identb], parts=[banks[bk]] if c else (), writes=[banks[bk]] if c == 0 else (), tick=(c == 7))
                dst = utok[hf][:, hb * 8:(hb + 1) * 8, jj * 128:(jj + 1) * 128]
                src = pb(bk).rearrange("p (c t) -> p c t", t=128)
                P.op("act", lambda e, dst=dst, src=src: e.copy(dst, src), reads=[banks[bk]], parts=[utokb[hf]])
            conv_chunk(sa, j, co, cob)
            P.op("act", lambda e, j=j: e.copy(x0T[:, j, :], co), reads=[cob], parts=[x0Tb])
        for b in wsl + [cx1b, cob, ubb]:
            P.free(b)
        dump("d_x0T", x0Tb, P.view(x0Tb))
        if "d_utok" in dbg:
            dump("d_utok", utokb[0], P.view(utokb[0]), 0)
            dump("d_utok", utokb[1], P.view(utokb[1]), 8192)
        checkpoint(2)

        cqb = P.alloc("cq_nT", 4 * S * 2)
        ckvb = P.alloc("ckv_nT", 2 * S * 2)
        kpeb = P.alloc("k_peT", S * 2)
        cq_nT = P.view(cqb, BF16, "p (c t) -> p c t", t=S)
        ckv_nT = P.view(ckvb, BF16, "p (c t) -> p c t", t=S)
        k_peT = P.view(kpeb)
        wmb = P.alloc("wm", 16 * 832 * 2)
        wm = P.view(wmb, BF16, "p (c n) -> p c n", n=832)
        for c4 in range(4):
            P.dma("pool", wm[:, c4 * 4:(c4 + 1) * 4, :], w_in[c4 * 512:(c4 + 1) * 512, 3072:3904].rearrange("(c p) n -> p c n", p=128),
                  parts=[wmb], sembuf=wmb)
        rtb = P.alloc("rope_tok", 16 * 2 * 64 * 4)
        rt = P.view(rtb, F32, "p (i k d) -> p i k d", k=2, d=64)
        P.dma("sp", P.view(rtb, F32), rope_tok[:, :], writes=[rtb], sembuf=rtb)
        gq_b, gq = load_gain("g_q")
        gkv_b, gkv = load_gain("g_kv")
        latb = [P.alloc("lat%d" % i, 896 * 2) for i in range(2)]
        t1b = P.alloc("t1", 64 * 4)
        t2b = P.alloc("t2", 64 * 4)
        t1 = P.view(t1b, F32)
        t2 = P.view(t2b, F32)
        def p2a_m(i):
                bA, bB, bT = (0, 1, 2) if i % 2 == 0 else (3, 4, 5)
                tok = slice(i * 128, (i + 1) * 128)
                for c in range(16):
                    P.op("pe", lambda e, c=c, tok=tok, bA=bA: e.matmul(pf(bA), lhsT=xnT[:, c, tok], rhs=wm[:, c, 0:512], start=(c == 0), stop=(c == 15)),
                         reads=[xnTb, wmb], writes=[banks[bA]] if c == 0 else (), parts=[banks[bA]] if c else (), tick=(c == 15))
                for c in range(16):
                    P.op("pe", lambda e, c=c, tok=tok, bB=bB: e.matmul(pf(bB, 0, 320), lhsT=xnT[:, c, tok], rhs=wm[:, c, 512:832], start=(c == 0), stop=(c == 15)),
                         reads=[xnTb, wmb], writes=[banks[bB]] if c == 0 else (), parts=[banks[bB]] if c else (), tick=(c == 15))

        def p2a_e(i):
                bA, bB, bT = (0, 1, 2) if i % 2 == 0 else (3, 4, 5)
                tok = slice(i * 128, (i + 1) * 128)
                lb = latb[i % 2]
                lat = P.view(lb)
                s1 = ss_col()
                P.op("act", lambda e, s1=s1, bA=bA: e.activation(junk[:, 0:512], pf(bA), AF.Square, accum_out=s1), reads=[banks[bA]], parts=[ssb])
                r1 = rstd_from_ss(s1, 1.0 / 512, ssb)
                P.op("dve", lambda e, lat=lat, r1=r1, bA=bA: e.scalar_tensor_tensor(lat[:, 0:512], pf(bA), r1, gq, op0=ALU.mult, op1=ALU.mult),
                     reads=[banks[bA], smallb, gq_b], parts=[lb])
                s2 = ss_col()
                P.op("act", lambda e, s2=s2, bB=bB: e.activation(junk[:, 0:256], pf(bB, 0, 256), AF.Square, accum_out=s2), reads=[banks[bB]], parts=[ssb])
                r2 = rstd_from_ss(s2, 1.0 / 256, ssb)
                P.op("dve", lambda e, lat=lat, r2=r2, bB=bB: e.scalar_tensor_tensor(lat[:, 512:768], pf(bB, 0, 256), r2, gkv, op0=ALU.mult, op1=ALU.mult),
                     reads=[banks[bB], smallb, gkv_b], parts=[lb])
                P.op("dve", lambda e, i=i, bB=bB: e.tensor_tensor(t1, pf(bB, 256, 64), rt[:, i, 0, :], op=ALU.mult), reads=[banks[bB], rtb], writes=[t1b])
                P.op("dve", lambda e, i=i, bB=bB: e.tensor_tensor(t2[:, 0:32], pf(bB, 288, 32), rt[:, i, 1, 0:32], op=ALU.mult), reads=[banks[bB], rtb], writes=[t2b])
                P.op("dve", lambda e, i=i, bB=bB: e.tensor_tensor(t2[:, 32:64], pf(bB, 256, 32), rt[:, i, 1, 32:64], op=ALU.mult), reads=[banks[bB], rtb], parts=[t2b])
                P.op("dve", lambda e, lat=lat: e.tensor_tensor(lat[:, 768:832], t1, t2, op=ALU.add), reads=[t1b, t2b], parts=[lb])
                P.op("dve", lambda e, lat=lat: e.tensor_tensor(lat[:, 832:896], t1, t2, op=ALU.add), reads=[t1b, t2b], parts=[lb])

        def p2a_t(i):
                bA, bB, bT = (0, 1, 2) if i % 2 == 0 else (3, 4, 5)
                tok = slice(i * 128, (i + 1) * 128)
                lb = latb[i % 2]
                lat = P.view(lb)
                for c in range(7):
                    P.op("pe", lambda e, c=c, lat=lat, bT=bT: e.transpose(pb(bT, c * 128, 128), lat[:, c * 128:(c + 1) * 128], ident),
                         reads=[lb, identb], writes=[banks[bT]] if c == 0 else (), parts=[banks[bT]] if c else (), tick=(c == 6))
                P.op("act", lambda e, tok=tok, bT=bT: e.copy(cq_nT[:, :, tok], pb(bT, 0, 512).rearrange("p (c t) -> p c t", t=128)), reads=[banks[bT]], parts=[cqb])
                P.op("act", lambda e, tok=tok, bT=bT: e.copy(ckv_nT[:, :, tok], pb(bT, 512, 256).rearrange("p (c t) -> p c t", t=128)), reads=[banks[bT]], parts=[ckvb])
                P.op("act", lambda e, tok=tok, bT=bT: e.copy(k_peT[:, tok], pb(bT, 768, 128)), reads=[banks[bT]], parts=[kpeb])

        p2a_m(0)
        p2a_e(0)
        for i in range(16):
            if i + 1 < 16:
                p2a_m(i + 1)
                p2a_e(i + 1)
            p2a_t(i)
        for b in [wmb, rtb, t1b, t2b, gq_b, gkv_b] + latb:
            P.free(b)
        P.free(xnTb)
        dump("d_cq", cqb, P.view(cqb))
        dump("d_ckv", ckvb, P.view(ckvb))
        dump("d_kpe", kpeb, P.view(kpeb))
        checkpoint(3)

        P.free(junkb)
        zTb = P.alloc("zT", S * 4)
        h1b = P.alloc("h1T", S * 4)
        h2b = P.alloc("h2T", S * 2)
        w3b = P.alloc("w3", 2048 * 2)
        fwb = P.alloc("fw12", 128 * 4)
        zTv = P.view(zTb, F32)
        h1T = P.view(h1b, F32)
        h2T = P.view(h2b)
        w3v = P.view(w3b)
        fw = P.view(fwb, F32)
        P.dma("sp", zTv[0:33, :], zT[:, :], writes=[zTb], sembuf=zTb)
        P.dma("pool", w3v[0:64, :], filt_w3[:, :], writes=[w3b], sembuf=w3b)
        P.dma("sp", fw[0:33, 0:64], filt_w1[:, :], parts=[fwb], sembuf=fwb)
        P.dma("sp", fw[0:64, 64:128], filt_w2[:, :], parts=[fwb], sembuf=fwb)
        argb = P.alloc("arg", S * 4)
        wrb = P.alloc("wrap", S * 4)
        arg = P.view(argb, F32)
        wr = P.view(wrb, F32)

        def sin_layer(lhsT, krows, rhs_ap, rhs_buf, dstT, dst_buf, bname, frname):
            for tg in range(4):
                P.op("pe", lambda e, tg=tg: e.matmul(pf(tg, 0, 512, 64), lhsT=lhsT, rhs=rhs_ap[0:krows, tg * 512:(tg + 1) * 512], start=True, stop=True),
                     reads=[fwb, rhs_buf], writes=[banks[tg]])
            raw = ps_f[0:64, 0:2048]
            P.op("dve", lambda e: e.tensor_scalar(arg[0:64, :], raw, pkc(bname)[0:64, :], pkc(frname)[0:64, :], op0=ALU.add, op1=ALU.mult),
                 reads=banks[0:4] + [cbuf], writes=[argb])
            P.op("dve", lambda e: e.tensor_scalar(wr[0:64, :], arg[0:64, :], PI, 2 * PI, op0=ALU.is_gt, op1=ALU.mult), reads=[argb], writes=[wrb])
            P.op("dve", lambda e: e.tensor_tensor(arg[0:64, :], arg[0:64, :], wr[0:64, :], op=ALU.subtract), reads=[wrb, argb], writes=[argb])
            P.op("dve", lambda e: e.tensor_scalar(wr[0:64, :], arg[0:64, :], -PI, 2 * PI, op0=ALU.is_lt, op1=ALU.mult), reads=[argb], writes=[wrb])
            P.op("dve", lambda e: e.tensor_tensor(arg[0:64, :], arg[0:64, :], wr[0:64, :], op=ALU.add), reads=[wrb, argb], writes=[argb])
            P.op("act", lambda e: e.activation(dstT[0:64, :], arg[0:64, :], AF.Sin), reads=[argb], writes=[dst_buf])

        sin_layer(fw[0:33, 0:64], 33, zTv, zTb, h1T, h1b, "fb1", "ffr1")
        sin_layer(fw[0:64, 64:128], 64, h1T, h1b, h2T, h2b, "fb2", "ffr2")
        for b in [zTb, h1b, argb, wrb]:
            P.free(b)
        hbb = P.alloc("hbias", 512 * 4)
        hbv = P.view(hbb, F32)

        yhb = [None, None]
        ssh = {}
        for hf in range(2):
            c0 = hf * 512
            P.dma("sp", hbv[0:1, :], hbias[:, c0:c0 + 512], writes=[hbb], sembuf=hbb)
            Ab = P.alloc("A", 16 * 512 * 2)
            Bb = P.alloc("Bm", 16 * 512 * 2)
            Av = P.view(Ab, BF16, "p (l c) -> p l c", c=512)
            Bv = P.view(Bb, BF16, "p (l c) -> p l c", c=512)
            winb = [P.alloc("win%d" % i, 512 * 4) for i in range(2)]
            tFb = P.alloc("tF", 512 * 4)
            tBb = P.alloc("tB", 512 * 4)
            tF = P.view(tFb, F32)
            tB = P.view(tBb, F32)
            for lc in range(16):
                wb_ = winb[lc % 2]
                wv_ = P.view(wb_, F32)
                P.dma("sp", wv_, window[lc * 128:(lc + 1) * 128, c0:c0 + 512], writes=[wb_], sembuf=wb_)
                bF, bB = (0, 1) if lc % 2 == 0 else (2, 3)
                P.op("pe", lambda e, lc=lc, bF=bF: e.matmul(pf(bF), lhsT=h2T[0:64, lc * 128:(lc + 1) * 128], rhs=w3v[0:64, c0:c0 + 512], start=True, stop=True),
                     reads=[h2b, w3b], writes=[banks[bF]])
                P.op("pe", lambda e, lc=lc, bB=bB: e.matmul(pf(bB), lhsT=h2T[0:64, lc * 128:(lc + 1) * 128], rhs=w3v[0:64, HW + c0:HW + c0 + 512], start=True, stop=True),
                     reads=[h2b, w3b], writes=[banks[bB]])
                P.op("dve", lambda e, wv_=wv_, bF=bF: e.tensor_tensor(tF, pf(bF), wv_, op=ALU.mult), reads=[banks[bF], wb_], writes=[tFb])
                P.op("dve", lambda e, wv_=wv_, bB=bB: e.tensor_tensor(tB, pf(bB), wv_, op=ALU.mult), reads=[banks[bB], wb_], writes=[tBb])
                P.op("dve", lambda e, lc=lc: e.tensor_tensor(Av[:, lc, :], tF, tB, op=ALU.add), reads=[tFb, tBb], parts=[Ab])
                P.op("dve", lambda e, lc=lc: e.tensor_tensor(Bv[:, lc, :], tB, tF, op=ALU.subtract), reads=[tFb, tBb], parts=[Bb])
                if lc == 0:
                    P.op("dve", lambda e: e.tensor_tensor(Av[0:1, 0, :], tF[0:1, :], hbv[0:1, :], op=ALU.add), reads=[tFb, hbb, Ab], parts=[Ab])
            for b in winb + [tFb, tBb]:
                P.free(b)
            if hf == 0:
                dump("d_A", Ab, P.view(Ab))
                if stop_after == 3.5:
                    checkpoint(3.5)
            Yb = P.alloc("Yre", 16 * 512 * 2)
            Zb = P.alloc("Z", 16 * 512 * 2)
            Yv = P.view(Yb, BF16, "p (f c) -> p f c", c=512)
            Zv = P.view(Zb, BF16, "p (f c) -> p f c", c=512)
            csb = [P.alloc("cs%d" % i, 2 * 16 * 128 * 2) for i in range(2)]
            kreb = P.alloc("kre", 512 * 4)
            kimb = P.alloc("kim", 512 * 4)
            pb_ = [P.alloc("p%d" % i, 512 * 4) for i in range(2)]
            kre = P.view(kreb, F32)
            kim = P.view(kimb, F32)
            pv = [P.view(b, F32) for b in pb_]
            uv = utok[hf]
            for fc in range(16):
                cb = csb[fc % 2]
                cv = P.view(cb, BF16, "p (k l j) -> p k l j", k=2, j=128)
                P.dma("sp", P.view(cb), dft_fwd[fc, :, :], writes=[cb], sembuf=cb)
                bs = 0 if fc % 2 == 0 else 4
                bKre, bKim, bUre, bUs = bs, bs + 1, bs + 2, bs + 3
                for lc in range(16):
                    st_, sp_ = (lc == 0), (lc == 15)
                    for (bk, k, rhs, rb) in ((bKre, 0, Av[:, lc, :], Ab), (bUre, 0, uv[:, lc, :], utokb[hf]), (bKim, 1, Bv[:, lc, :], Bb), (bUs, 1, uv[:, lc, :], utokb[hf])):
                        P.op("pe", lambda e, bk=bk, k=k, lc=lc, rhs=rhs, st_=st_, sp_=sp_, cv=cv: e.matmul(pf(bk), lhsT=cv[:, k, lc, :], rhs=rhs, start=st_, stop=sp_),
                             reads=[cb, rb], writes=[banks[bk]] if st_ else (), parts=() if st_ else [banks[bk]], tick=sp_)
                P.op("act", lambda e, bKre=bKre: e.copy(kre, pf(bKre)), reads=[banks[bKre]], writes=[kreb])
                P.op("act", lambda e, bKim=bKim: e.copy(kim, pf(bKim)), reads=[banks[bKim]], writes=[kimb])
                P.op("dve", lambda e, bUre=bUre: e.tensor_tensor(pv[0], pf(bUre), kre, op=ALU.mult), reads=[banks[bUre], kreb], writes=[pb_[0]])
                P.op("dve", lambda e, bUs=bUs: e.tensor_tensor(pv[1], pf(bUs), kim, op=ALU.mult), reads=[banks[bUs], kimb], writes=[pb_[1]])
                P.op("dve", lambda e, fc=fc: e.tensor_tensor(Yv[:, fc, :], pv[0], pv[1], op=ALU.add), reads=[pb_[0], pb_[1]], parts=[Yb])
                P.op("dve", lambda e, bUs=bUs: e.tensor_tensor(pv[0], pf(bUs), kre, op=ALU.mult), reads=[banks[bUs], kreb], writes=[pb_[0]])
                P.op("dve", lambda e, bUre=bUre: e.tensor_tensor(pv[1], pf(bUre), kim, op=ALU.mult), reads=[banks[bUre], kimb], writes=[pb_[1]])
                P.op("dve", lambda e, fc=fc: e.tensor_tensor(Zv[:, fc, :], pv[0], pv[1], op=ALU.subtract), reads=[pb_[0], pb_[1]], parts=[Zb])
            for b in csb + [kreb, kimb] + pb_ + [Ab, Bb]:
                P.free(b)
            P.free(utokb[hf])
            if hf == 0:
                ssh["b"] = P.alloc("ss_h", S * 4)
            ssh_b = ssh["b"]
            ss_h = P.view(ssh_b, F32)
            yhb[hf] = P.alloc("yhT%d" % hf, 4 * S * 2)
            yhv = P.view(yhb[hf], BF16, "p (c t) -> p c t", t=S)
            ctb = [P.alloc("ct%d" % i, 2 * 16 * 256 * 2) for i in range(2)]
            yxb = [P.alloc("yx%d" % i, 256 * 4) for i in range(2)]
            sqb = [P.alloc("sq%d" % i, 256 * 2) for i in range(2)]
            it = 0
            pend_inv = []
            for tg in range(8):
                tb = ctb[tg % 2]
                tv = P.view(tb, BF16, "p (k f j) -> p k f j", k=2, j=256)
                P.dma("sp", P.view(tb), dft_inv[tg, :, :], writes=[tb], sembuf=tb)
                tsl = slice(tg * 256, (tg + 1) * 256)
                bS = 6 + (tg % 2)
                for cc in range(4):
                    bk = it % 4
                    yx_b = yxb[it % 2]
                    sq_b = sqb[it % 2]
                    yx = P.view(yx_b, F32)
                    sq = P.view(sq_b)
                    it += 1
                    for fc in range(16):
                        P.op("pe", lambda e, bk=bk, fc=fc, cc=cc, tv=tv: e.matmul(pf(bk, 0, 256), lhsT=Yv[:, fc, cc * 128:(cc + 1) * 128], rhs=tv[:, 0, fc, :], start=(fc == 0), stop=False),
                             reads=[Yb, tb], writes=[banks[bk]] if fc == 0 else (), parts=[banks[bk]] if fc else (), tick=False)
                        P.op("pe", lambda e, bk=bk, fc=fc, cc=cc, tv=tv: e.matmul(pf(bk, 0, 256), lhsT=Zv[:, fc, cc * 128:(cc + 1) * 128], rhs=tv[:, 1, fc, :], start=False, stop=(fc == 15)),
                             reads=[Zb, tb], parts=[banks[bk]], tick=(fc == 15))
                    ch = hf * 4 + cc
                    for f_ in pend_inv:
                        f_()
                    pend_inv.clear()
                    P.op("dve", lambda e, bk=bk, ch=ch, tsl=tsl, yx=yx: e.tensor_tensor(yx, pf(bk, 0, 256), x0T[:, ch, tsl], op=ALU.mult), reads=[banks[bk], x0Tb], writes=[yx_b])
                    P.op("act", lambda e, yx=yx, sq=sq: e.activation(sq, yx, AF.Square), reads=[yx_b], writes=[sq_b])
                    P.op("act", lambda e, yx=yx, cc=cc, tsl=tsl, ch=ch, yhv=yhv: e.activation(yhv[:, cc, tsl], yx, AF.Copy, scale=pkc("hgain", ch, 1)), reads=[yx_b, cbuf], parts=[yhb[hf]])

                    def ones_mm(sq=sq, sq_b=sq_b, bS=bS, cc=cc, tsl=tsl, hf=hf):
                        P.op("pe", lambda e: e.matmul(pf(bS, 0, 256), lhsT=ones, rhs=sq, start=(cc == 0), stop=(cc == 3)),
                             reads=[sq_b, onesb], writes=[banks[bS]] if cc == 0 else (), parts=[banks[bS]] if cc else ())
                        if cc == 3:
                            if hf == 0:
                                P.op("dve", lambda e: e.tensor_copy(ss_h[:, tsl], pf(bS, 0, 256)), reads=[banks[bS]], parts=[ssh_b])
                            else:
                                P.op("dve", lambda e: e.tensor_tensor(ss_h[:, tsl], ss_h[:, tsl], pf(bS, 0, 256), op=ALU.add), reads=[banks[bS], ssh_b], parts=[ssh_b])
                    pend_inv.append(ones_mm)
            for f_ in pend_inv:
                f_()
            pend_inv.clear()
            for b in ctb + yxb + sqb + [Yb, Zb]:
                P.free(b)
        for b in [h2b, w3b, fwb, hbb, x0Tb]:
            P.free(b)
        P.op("act", lambda e: e.activation(ss_h, ss_h, AF.Sqrt, bias=pk_eps, scale=1.0 / HW), reads=[ssh_b, epsb], writes=[ssh_b])
        P.op("dve", lambda e: e.reciprocal(ss_h, ss_h), reads=[ssh_b], writes=[ssh_b])
        for hf in range(2):
            yhv = P.view(yhb[hf], BF16, "p (c t) -> p c t", t=S)
            for cc in range(4):
                P.op("dve", lambda e, yhv=yhv, cc=cc: e.tensor_tensor(yhv[:, cc, :], yhv[:, cc, :], ss_h, op=ALU.mult), reads=[ssh_b, yhb[hf]], writes=[yhb[hf]])
        P.free(ssh_b)
        if "d_yh" in dbg:
            for hf in range(2):
                dump("d_yh", yhb[hf], P.view(yhb[hf]), hf * 4 * S)
        checkpoint(4)

        Vb = P.alloc("V", 16 * 1024 * 2)
        Vv = P.view(Vb, BF16, "p (k c) -> p k c", c=1024)
        wvb = P.alloc("wv", 2 * 1024 * 2)
        wvv = P.view(wvb, BF16, "p (c h d) -> p c h d", c=2, d=128)
        wkv_r = w_ukv.rearrange("(c p) (h t) -> p c h t", p=128, t=256)
        for c in range(2):
            P.dma("pool", wvv[:, c, :, :], wkv_r[:, c, :, 128:256], parts=[wvb], sembuf=wvb)
        wvf = P.view(wvb, BF16, "p (c n) -> p c n", c=2)
        for kc in range(16):
            for hh in range(2):
                bk = (kc * 2 + hh) % 4
                for c in range(2):
                    P.op("pe", lambda e, bk=bk, kc=kc, c=c, hh=hh: e.matmul(pf(bk), lhsT=ckv_nT[:, c, kc * 128:(kc + 1) * 128], rhs=wvf[:, c, hh * 512:(hh + 1) * 512], start=(c == 0), stop=(c == 1)),
                         reads=[ckvb, wvb], writes=[banks[bk]] if c == 0 else (), parts=[banks[bk]] if c else (), tick=(c == 1))
                if hh == 0:
                    P.op("act", lambda e, bk=bk, kc=kc, hh=hh: e.copy(Vv[:, kc, hh * 512:(hh + 1) * 512], pf(bk)), reads=[banks[bk]], parts=[Vb])
                else:
                    P.op("dve", lambda e, bk=bk, kc=kc, hh=hh: e.tensor_copy(Vv[:, kc, hh * 512:(hh + 1) * 512], pf(bk)), reads=[banks[bk]], parts=[Vb])
        P.free(wvb)
        rfb = P.alloc("rope_feat", 2 * S * 2)
        rf = P.view(rfb, BF16, "p (k t) -> p k t", k=2)
        P.dma("sp", P.view(rfb)[0:64, :], rope_feat[:, :], writes=[rfb], sembuf=rfb)
        yab = P.alloc("y_attnT", 8 * S * 2)
        yav = P.view(yab, BF16, "p (h t) -> p h t", t=S)
        ssa_b = P.alloc("ss_a", S * 4)
        ss_a = P.view(ssa_b, F32)
        hwb = [P.alloc("hw%d" % i, (4 * 256 + 2 * 128) * 2) for i in range(2)]
        qnb = [P.alloc("qn%d" % i, S * 2) for i in range(2)]
        qpb = [P.alloc("qp%d" % i, S * 2) for i in range(2)]
        knb = [P.alloc("kn%d" % i, S * 2) for i in range(2)]
        ptb = [P.alloc("pT%d" % i, 512 * 2) for i in range(6)]
        densb = P.alloc("den_sb", 512 * 4)
        den_sb = P.view(densb, F32)
        r1b = P.alloc("r1", 512 * 4)
        r2b = P.alloc("r2", 512 * 4)
        sq2b = P.alloc("sq2", 512 * 2)
        rp1b = P.alloc("rp1", 512 * 4)
        rp2b = P.alloc("rp2", 512 * 4)
        r1 = P.view(r1b, F32)
        r2 = P.view(r2b, F32)
        sq2 = P.view(sq2b)
        rp1 = P.view(rp1b, F32)
        rp2 = P.view(rp2b, F32)
        QSCALE = (128 + 64) ** -0.5
        gen_rr = [0]

        def gen_bank():
            return 7

        ssacc_b = P.alloc("ssacc", S * 4)
        ssacc = P.view(ssacc_b, F32)
        onesfb = P.alloc("ones_f", 128 * 4)
        ones_f = P.view(onesfb, F32)
        P.op("dve", lambda e: e.memset(ones_f, 1.0), writes=[onesfb])

        def gen_head(h):
            sl = h % 2
            hb_ = hwb[sl]
            hv = P.view(hb_)
            wq = hv[:, 0:1024].rearrange("p (c n) -> p c n", n=256)
            wk = hv[:, 1024:1280].rearrange("p (c n) -> p c n", n=128)
            q_r = w_uq.rearrange("(c p) n -> p c n", p=128)
            k_r = w_ukv.rearrange("(c p) n -> p c n", p=128)
            P.dma("pool", wq[:, :, 0:192], q_r[:, :, h * 192:(h + 1) * 192], parts=[hb_], sembuf=hb_)
            P.dma("pool", wq[:, :, 192:224], q_r[:, :, h * 192 + 160:h * 192 + 192], parts=[hb_], sembuf=hb_)
            P.dma("pool", wq[:, :, 224:256], q_r[:, :, h * 192 + 128:h * 192 + 160], parts=[hb_], sembuf=hb_)
            P.dma("pool", wk, k_r[:, :, h * 256:h * 256 + 128], parts=[hb_], sembuf=hb_)
            qn = P.view(qnb[sl])
            qp = P.view(qpb[sl])
            kn = P.view(knb[sl])
            for tg in range(4):
                ts_ = slice(tg * 512, (tg + 1) * 512)
                bk = gen_bank()
                for c in range(4):
                    P.op("pe", lambda e, bk=bk, c=c, ts_=ts_, wq=wq: e.matmul(pf(bk), lhsT=wq[:, c, 0:128], rhs=cq_nT[:, c, ts_], start=(c == 0), stop=(c == 3)),
                         reads=[hb_, cqb], writes=[banks[bk]] if c == 0 else (), parts=[banks[bk]] if c else (), tick=(c == 3))
                P.op("dve", lambda e, bk=bk, ts_=ts_, qn=qn: e.tensor_scalar(qn[:, ts_], pf(bk), QSCALE, None, op0=ALU.mult), reads=[banks[bk]], parts=[qnb[sl]])
                yield
                bk = gen_bank()
                for c in range(2):
                    P.op("pe", lambda e, bk=bk, c=c, ts_=ts_, wk=wk: e.matmul(pf(bk), lhsT=wk[:, c, :], rhs=ckv_nT[:, c, ts_], start=(c == 0), stop=(c == 1)),
                         reads=[hb_, ckvb], writes=[banks[bk]] if c == 0 else (), parts=[banks[bk]] if c else (), tick=(c == 1))
                P.op("dve", lambda e, bk=bk, ts_=ts_, kn=kn: e.tensor_copy(kn[:, ts_], pf(bk)), reads=[banks[bk]], parts=[knb[sl]])
                yield
                bk = gen_bank()
                for c in range(4):
                    P.op("pe", lambda e, bk=bk, c=c, ts_=ts_, wq=wq: e.matmul(pf(bk, 0, 512, 64), lhsT=wq[:, c, 128:192], rhs=cq_nT[:, c, ts_], start=(c == 0), stop=(c == 3)),
                         reads=[hb_, cqb], writes=[banks[bk]] if c == 0 else (), parts=[banks[bk]] if c else (), tick=(c == 3))
                P.op("dve", lambda e, bk=bk, ts_=ts_: e.tensor_tensor(rp1[0:64, :], pf(bk, 0, 512, 64), rf[0:64, 0, ts_], op=ALU.mult), reads=[banks[bk], rfb], writes=[rp1b])
                yield
                bk2 = gen_bank()
                for c in range(4):
                    P.op("pe", lambda e, bk2=bk2, c=c, ts_=ts_, wq=wq: e.matmul(pf(bk2, 0, 512, 64), lhsT=wq[:, c, 192:256], rhs=cq_nT[:, c, ts_], start=(c == 0), stop=(c == 3)),
                         reads=[hb_, cqb], writes=[banks[bk2]] if c == 0 else (), parts=[banks[bk2]] if c else (), tick=(c == 3))
                P.op("dve", lambda e, bk2=bk2, ts_=ts_: e.tensor_tensor(rp2[0:64, :], pf(bk2, 0, 512, 64), rf[0:64, 1, ts_], op=ALU.mult), reads=[banks[bk2], rfb], writes=[rp2b])
                P.op("dve", lambda e, ts_=ts_, qp=qp: e.tensor_tensor(qp[0:64, ts_], rp1[0:64, :], rp2[0:64, :], op=ALU.add), reads=[rp1b, rp2b], parts=[qpb[sl]])
                yield

        att_it = [0]

        def attend_head(h, side):
            sl = h % 2
            qn = P.view(qnb[sl])
            qp = P.view(qpb[sl])
            kn = P.view(knb[sl])
            for qg in range(4):
                qs = slice(qg * 512, (qg + 1) * 512)
                bO = 4 + att_it[0] % 2
                bD = 6
                att_it[0] += 1

                def score(kc):
                    bk = kc % 4
                    ks = slice(kc * 128, (kc + 1) * 128)
                    P.op("pe", lambda e, bk=bk, ks=ks: e.matmul(pf(bk), lhsT=kn[:, ks], rhs=qn[:, qs], start=True, stop=False),
                         reads=[knb[sl], qnb[sl]], writes=[banks[bk]], tick=False)
                    P.op("pe", lambda e, bk=bk, ks=ks: e.matmul(pf(bk), lhsT=k_peT[0:64, ks], rhs=qp[0:64, qs], start=False, stop=True),
                         reads=[kpeb, qpb[sl]], parts=[banks[bk]])
                    pt_b = ptb[kc % 6]
                    pt = P.view(pt_b)
                    P.op("act", lambda e, bk=bk, pt=pt: e.activation(pt, pf(bk), AF.Exp), reads=[banks[bk]], writes=[pt_b])

                def pv_(kc):
                    pt_b = ptb[kc % 6]
                    pt = P.view(pt_b)
                    P.op("pe", lambda e, kc=kc, pt=pt: e.matmul(pf(bO), lhsT=Vv[:, kc, h * 128:(h + 1) * 128], rhs=pt, start=(kc == 0), stop=(kc == 15)),
                         reads=[Vb, pt_b], writes=[banks[bO]] if kc == 0 else (), parts=[banks[bO]] if kc else (), tick=False)
                    P.op("pe", lambda e, kc=kc, pt=pt: e.matmul(pf(bD), lhsT=ones, rhs=pt, start=(kc == 0), stop=(kc == 15)),
                         reads=[onesb, pt_b], writes=[banks[bD]] if kc == 0 else (), parts=[banks[bD]] if kc else ())

                for kc in range(4):
                    score(kc)
                for kc in range(0, 16, 2):
                    P.prewait("pe", reads=[ptb[(kc + 1) % 6]])
                    pv_(kc)
                    pv_(kc + 1)
                    if kc + 4 < 16:
                        score(kc + 4)
                        score(kc + 5)
                    if side is not None and kc >= 8:
                        next(side, None)
                P.op("dve", lambda e: e.tensor_copy(den_sb, pf(bD)), reads=[banks[bD]], writes=[densb])
                P.op("dve", lambda e: e.reciprocal(r1, den_sb), reads=[densb], writes=[r1b])
                P.op("dve", lambda e: e.tensor_tensor(r2, pf(bO), r1, op=ALU.mult), reads=[banks[bO], r1b], writes=[r2b])
                P.op("dve", lambda e: e.tensor_tensor(sq2, r2, r2, op=ALU.mult), reads=[r2b], writes=[sq2b])
                P.op("dve", lambda e, qs=qs: e.tensor_scalar(yav[:, h, qs], r2, pkc("again", h, 1), None, op0=ALU.mult), reads=[r2b, cbuf], parts=[yab])
                if h == 0:
                    P.op("dve", lambda e, qs=qs: e.tensor_copy(ssacc[:, qs], sq2), reads=[sq2b], parts=[ssacc_b])
                else:
                    P.op("dve", lambda e, qs=qs: e.tensor_tensor(ssacc[:, qs], ssacc[:, qs], sq2, op=ALU.add), reads=[sq2b, ssacc_b], parts=[ssacc_b])

        for _ in gen_head(0):
            pass
        for h in range(NH):
            side = gen_head(h + 1) if h + 1 < NH else None
            attend_head(h, side)
            if side is not None:
                for _ in side:
                    pass
        for qg in range(4):
            P.op("pe", lambda e, qg=qg: e.matmul(pf(qg), lhsT=ones_f, rhs=ssacc[:, qg * 512:(qg + 1) * 512], start=True, stop=True),
                 reads=[onesfb, ssacc_b], writes=[banks[qg]])
            P.op("dve", lambda e, qg=qg: e.tensor_copy(ss_a[:, qg * 512:(qg + 1) * 512], pf(qg)), reads=[banks[qg]], parts=[ssa_b])
        for b in [ssacc_b, onesfb]:
            P.free(b)
        for b in hwb + qnb + qpb + knb + ptb + [densb, r1b, r2b, sq2b, rp1b, rp2b, rfb, Vb, cqb, ckvb, kpeb]:
            P.free(b)
        P.op("act", lambda e: e.activation(ss_a, ss_a, AF.Sqrt, bias=pk_eps, scale=1.0 / HW), reads=[ssa_b, epsb], writes=[ssa_b])
        P.op("dve", lambda e: e.reciprocal(ss_a, ss_a), reads=[ssa_b], writes=[ssa_b])
        for h in range(NH):
            P.op("dve", lambda e, h=h: e.tensor_tensor(yav[:, h, :], yav[:, h, :], ss_a, op=ALU.mult), reads=[ssa_b, yab], writes=[yab])
        P.free(ssa_b)
        dump("d_ya", yab, P.view(yab))
        checkpoint(5)

        junkb = P.alloc("junk", 2048 * 2)
        junk = P.view(junkb)
        gpost_b, gpost = load_gain("g_post")
        gffn_b, gffn = load_gain("g_ffn")
        wob = P.alloc("w_out", 16 * D * 2)
        wo = P.view(wob, BF16, "p (k n) -> p k n", n=D)
        for k4 in range(4):
            P.dma("pool", wo[:, k4 * 4:(k4 + 1) * 4, :], w_out[k4 * 512:(k4 + 1) * 512, :].rearrange("(k p) n -> p k n", p=128), parts=[wob], sembuf=wob)
        xts = [P.alloc("xt%d" % i, D * 4) for i in range(2)]
        hts = [P.alloc("ht%d" % i, D * 4) for i in range(2)]
        hnbs = [P.alloc("hnb%d" % i, D * 2) for i in range(2)]
        hsts = [P.alloc("hst%d" % i, 16 * 128 * 2) for i in range(2)]
        ychunks = [Buf("ychunk%d" % i) for i in range(16)]
        hn_half = [Buf("hnscr%d" % i) for i in range(2)]
        yh_views = [P.view(yhb[hf], BF16, "p (c t) -> p c t", t=S) for hf in range(2)]

        def mixT(kc, tok):
            if kc < 4:
                return yh_views[0][:, kc, tok], yhb[0]
            if kc < 8:
                return yh_views[1][:, kc - 4, tok], yhb[1]
            return yav[:, kc - 8, tok], yab

        def w_mm(i):
            tok = slice(i * 128, (i + 1) * 128)
            bs = (i % 2) * 4
            xt_b = xts[i % 2]
            P.dma("sp", P.view(xt_b, F32), x[tok, :], writes=[xt_b], sembuf=xt_b)
            for dg in range(4):
                for kc in range(16):
                    lt, lb_ = mixT(kc, tok)
                    P.op("pe", lambda e, dg=dg, kc=kc, lt=lt, bs=bs: e.matmul(pf(bs + dg), lhsT=lt, rhs=wo[:, kc, dg * 512:(dg + 1) * 512], start=(kc == 0), stop=(kc == 15)),
                         reads=[lb_, wob], writes=[banks[bs + dg]] if kc == 0 else (), parts=[banks[bs + dg]] if kc else (), tick=(kc == 15))

        def w_evac(i):
            tok = slice(i * 128, (i + 1) * 128)
            bs = (i % 2) * 4
            xt_b, ht_b, hn_b = xts[i % 2], hts[i % 2], hnbs[i % 2]
            xt = P.view(xt_b, F32)
            ht = P.view(ht_b, F32)
            hnv = P.view(hn_b)
            mixed = ps_f[:, bs * 512:bs * 512 + 2048]
            bset = banks[bs:bs + 4]
            s1 = ss_col()
            P.op("act", lambda e, s1=s1: e.activation(junk, mixed, AF.Square, accum_out=s1), reads=bset, parts=[ssb])
            rs = rstd_from_ss(s1, 1.0 / D, ssb)
            P.op("dve", lambda e, ht=ht: e.tensor_tensor(ht, mixed, gpost, op=ALU.mult), reads=bset + [gpost_b], writes=[ht_b])
            P.op("dve", lambda e, ht=ht, rs=rs, xt=xt: e.scalar_tensor_tensor(ht, ht, rs, xt, op0=ALU.mult, op1=ALU.add), reads=[smallb, xt_b, ht_b], writes=[ht_b])
            P.dma("sp", y[tok, :], ht, reads=[ht_b], writes=[ychunks[i]], sembuf=ht_b)
            s2 = ss_col()
            P.op("act", lambda e, s2=s2, ht=ht: e.activation(junk, ht, AF.Square, accum_out=s2), reads=[ht_b], parts=[ssb])
            rs2 = rstd_from_ss(s2, 1.0 / D, ssb)
            P.op("dve", lambda e, ht=ht, rs2=rs2, hnv=hnv: e.scalar_tensor_tensor(hnv, ht, rs2, gffn, op0=ALU.mult, op1=ALU.mult), reads=[ht_b, smallb, gffn_b], writes=[hn_b])

        def w_tr(i):
            tok = slice(i * 128, (i + 1) * 128)
            bs = (i % 2) * 4
            hn_b, hs_b = hnbs[i % 2], hsts[i % 2]
            hnv = P.view(hn_b)
            hst = P.view(hs_b, BF16, "p (c t) -> p c t", t=128)
            for hb in range(2):
                bk = bs + hb
                for c in range(8):
                    cc = hb * 8 + c
                    P.op("pe", lambda e, bk=bk, c=c, cc=cc, hnv=hnv: e.transpose(pb(bk, c * 128, 128), hnv[:, cc * 128:(cc + 1) * 128], ident),
                         reads=[hn_b, identb], parts=[banks[bk]] if c else (), writes=[banks[bk]] if c == 0 else (), tick=(c == 7))
                P.op("act", lambda e, bk=bk, hb=hb, hst=hst: e.copy(hst[:, hb * 8:(hb + 1) * 8, :], pb(bk).rearrange("p (c t) -> p c t", t=128)), reads=[banks[bk]], parts=[hs_b])
            P.dma("sp", hn_scr[:, :, tok], hst, reads=[hs_b], parts=[hn_half[i // 8]], sembuf=hs_b)

        w_mm(0)
        for i in range(16):
            w_evac(i)
            if i + 1 < 16:
                w_mm(i + 1)
            w_tr(i)
        for b in xts + hts + hnbs + hsts + [wob, yab, yhb[0], yhb[1], gpost_b, gffn_b]:
            P.free(b)

        T = 1024
        actb = P.alloc("act", NFF * T * 2)
        actv = P.view(actb, BF16, "p (j t) -> p j t", t=T)
        hnTb = P.alloc("hnT", 16 * (T + 2) * 2)
        hnT = P.view(hnTb, BF16, "p (c t) -> p c t", t=T + 2)
        fbb = hnTb
        fbv = P.view(hnTb)[:, 0:8 * D].rearrange("p (i d) -> p i d", d=D)
        gpffn_b, gpffn = load_gain("g_pffn")
        wgub = [P.alloc("wgu%d" % i, 2 * 16 * 128 * 2) for i in range(2)]
        wdb = [P.alloc("wd%d" % i, 4 * 512 * 2) for i in range(3)]
        cob2 = [P.alloc("co%d" % i, T * 4) for i in range(2)]
        htin = [P.alloc("htin%d" % i, D * 4) for i in range(3)]
        ssq_b = P.alloc("ssq", 8 * 4 * 4)
        ssq = P.view(ssq_b, F32)
        for tile in range(2):
            T0 = tile * T
            if tile == 0:
                P.op("dve", lambda e: e.memset(hnT[:, :, 0:1], 0.0), parts=[hnTb])
                P.dma("sp", hnT[:, 0:8, 1:T + 2], hn_scr[:, 0:8, 0:T + 1], reads=hn_half, parts=[hnTb], sembuf=hnTb)
                P.dma("pool", hnT[:, 8:16, 1:T + 2], hn_scr[:, 8:16, 0:T + 1], reads=hn_half, parts=[hnTb], sembuf=hnTb)
            else:
                P.op("dve", lambda e: e.memset(hnT[:, :, T + 1:T + 2], 0.0), parts=[hnTb])
                P.dma("sp", hnT[:, 0:8, 0:T + 1], hn_scr[:, 0:8, T0 - 1:S], reads=hn_half, parts=[hnTb], sembuf=hnTb)
                P.dma("pool", hnT[:, 8:16, 0:T + 1], hn_scr[:, 8:16, T0 - 1:S], reads=hn_half, parts=[hnTb], sembuf=hnTb)
            for j in range(NFF):
                wb_ = wgub[j % 2]
                wgu = P.view(wb_, BF16, "p (k c n) -> p k c n", k=2, n=128)
                up_r = w_up.rearrange("(c p) n -> p c n", p=128)
                P.dma("pool", wgu[:, 0, :, :], up_r[:, :, j * 128:(j + 1) * 128], parts=[wb_], sembuf=wb_)
                P.dma("pool", wgu[:, 1, :, :], up_r[:, :, DFF + j * 128:DFF + (j + 1) * 128], parts=[wb_], sembuf=wb_)
                gs = 0 if j % 2 == 0 else 3
                for (off, n_) in ((0, 512), (512, 512), (1024, 2)):
                    bk = gs + off // 512
                    for c in range(16):
                        P.op("pe", lambda e, bk=bk, c=c, off=off, n_=n_, wgu=wgu: e.matmul(pf(bk, 0, n_), lhsT=wgu[:, 0, c, :], rhs=hnT[:, c, off:off + n_], start=(c == 0), stop=(c == 15)),
                             reads=[wb_, hnTb], writes=[banks[bk]] if c == 0 else (), parts=[banks[bk]] if c else (), tick=(c == 15))
                for g in range(2):
                    bk = 6 + g
                    for c in range(16):
                        P.op("pe", lambda e, bk=bk, c=c, g=g, wgu=wgu: e.matmul(pf(bk), lhsT=wgu[:, 1, c, :], rhs=hnT[:, c, 1 + g * 512:1 + (g + 1) * 512], start=(c == 0), stop=(c == 15)),
                             reads=[wb_, hnTb], writes=[banks[bk]] if c == 0 else (), parts=[banks[bk]] if c else (), tick=(c == 15))
                raw = ps_f[:, gs * 512:gs * 512 + T + 2]
                gset = banks[gs:gs + 3]
                co_b = cob2[j % 2]
                cov = P.view(co_b, F32)
                w0 = pkc("fconv_w", 0 * NFF + j, 1)
                w1 = pkc("fconv_w", 1 * NFF + j, 1)
                w2 = pkc("fconv_w", 2 * NFF + j, 1)
                bb = pkc("fconv_b", j, 1)
                P.op("act", lambda e, cov=cov, raw=raw, bb=bb, w1=w1: e.activation(cov, raw[:, 1:T + 1], AF.Identity, bias=bb, scale=w1), reads=gset + [cbuf], writes=[co_b])
                P.op("dve", lambda e, cov=cov, raw=raw, w0=w0: e.scalar_tensor_tensor(cov, raw[:, 0:T], w0, cov, op0=ALU.mult, op1=ALU.add), reads=gset + [cbuf, co_b], writes=[co_b])
                P.op("dve", lambda e, cov=cov, raw=raw, w2=w2: e.scalar_tensor_tensor(cov, raw[:, 2:T + 2], w2, cov, op0=ALU.mult, op1=ALU.add), reads=gset + [cbuf, co_b], writes=[co_b])
                P.op("act", lambda e, cov=cov: e.activation(cov, cov, AF.Gelu_apprx_tanh), reads=[co_b], writes=[co_b])
                P.op("dve", lambda e, cov=cov, j=j: e.tensor_tensor(actv[:, j, :], cov, ps_f[:, 6 * 512:8 * 512], op=ALU.mult), reads=[co_b, banks[6], banks[7]], parts=[actb])
            wd_r = w_down.rearrange("(k p) n -> p k n", p=128)
            wi = 0
            for dg in range(4):
                for k4 in range(NFF // 4):
                    wb_ = wdb[wi % 3]
                    wi += 1
                    wdv = P.view(wb_, BF16, "p (k n) -> p k n", n=512)
                    P.dma("pool", wdv, wd_r[:, k4 * 4:(k4 + 1) * 4, dg * 512:(dg + 1) * 512], writes=[wb_], sembuf=wb_)
                    for kk in range(4):
                        kc = k4 * 4 + kk
                        for tc in range(8):
                            P.op("pe", lambda e, tc=tc, kc=kc, kk=kk, wdv=wdv: e.matmul(pf(tc), lhsT=actv[:, kc, tc * 128:(tc + 1) * 128], rhs=wdv[:, kk, :], start=(kc == 0), stop=(kc == NFF - 1)),
                                 reads=[actb, wb_], writes=[banks[tc]] if kc == 0 else (), parts=[banks[tc]] if kc else (), tick=(kc == NFF - 1 or (kk == 3 and tc == 7)))
                for tc in range(8):
                    col = ssq[:, tc * 4 + dg:tc * 4 + dg + 1]
                    P.op("act", lambda e, tc=tc, col=col: e.activation(junk[:, 0:512], pf(tc), AF.Square, accum_out=col), reads=[banks[tc]], writes=[junkb], parts=[ssq_b])
                    P.op("dve", lambda e, tc=tc, dg=dg: e.tensor_copy(fbv[:, tc, dg * 512:(dg + 1) * 512], pf(tc)), reads=[banks[tc]], parts=[fbb])
            for tc in range(8):
                gi = tile * 8 + tc
                tok = slice(gi * 128, (gi + 1) * 128)
                hb_ = htin[tc % 3]
                hv = P.view(hb_, F32)
                P.dma("sp", hv, y[tok, :], reads=[ychunks[gi]], writes=[hb_], sembuf=hb_)
                s1 = ss_col()
                P.op("dve", lambda e, s1=s1, tc=tc: e.tensor_reduce(s1, ssq[:, tc * 4:tc * 4 + 4], axis=mybir.AxisListType.X, op=ALU.add), reads=[ssq_b], parts=[ssb])
                rs = rstd_from_ss(s1, 1.0 / D, ssb)
                for hh in range(2):
                    tb_ = cob2[hh]
                    tv_ = P.view(tb_, F32)
                    cs_ = slice(hh * 1024, (hh + 1) * 1024)
                    P.op("dve", lambda e, tc=tc, rs=rs, tv_=tv_, cs_=cs_: e.scalar_tensor_tensor(tv_, fbv[:, tc, cs_], rs, gpffn[:, cs_], op0=ALU.mult, op1=ALU.mult), reads=[fbb, smallb, gpffn_b], writes=[tb_])
                    P.op("dve", lambda e, hv=hv, tv_=tv_, cs_=cs_: e.tensor_tensor(hv[:, cs_], hv[:, cs_], tv_, op=ALU.add), reads=[tb_, hb_], writes=[hb_])
                P.dma("pool", y[tok, :], hv, reads=[hb_], writes=[ychunks[gi]], sembuf=hb_)
        if not P.dead:
            P.wait_all("sp", ychunks + P.dumps)

        block = es.enter_context(nc.Block())

        @block.tensor
        def _(e):
            for f in P.streams["pe"]:
                f(e)

        @block.scalar
        def _(e):
            for f in P.streams["act"]:
                f(e)

        @block.vector
        def _(e):
            for f in P.streams["dve"]:
                f(e)

        @block.gpsimd
        def _(e):
            for f in P.streams["pool"]:
                f(e)

        @block.sync
        def _(e):
            for f in P.streams["sp"]:
                f(e)

        build_program.stats = dict(peak=P.peak, nsem=P.nsem, n={k: len(v) for k, v in P.streams.items()})
    return nc


_NC_CACHE = {}


def _pack_params(inp):
    def col(v, nch):
        return np.ascontiguousarray(np.asarray(v, np.float32).reshape(nch, 128).T)

    def bc(v):
        v = np.asarray(v, np.float32).reshape(1, -1)
        return np.ascontiguousarray(np.broadcast_to(v, (128, v.shape[1])))

    def pad64(v):
        o = np.zeros((128, 1), np.float32)
        o[:64, 0] = np.asarray(v, np.float32)
        return o

    hcw = np.asarray(inp["hyena_conv_w"][0], np.float32)
    fcw = np.asarray(inp["ffn_conv_w"][0], np.float32)
    parts = [
        np.concatenate([col(hcw[k], 24) for k in range(3)], axis=1), col(inp["hyena_conv_b"][0], 24),
        col(inp["hyena_out_gain"][0], 8), col(inp["attn_out_gain"][0], 8),
        np.concatenate([col(fcw[k], NFF) for k in range(3)], axis=1), col(inp["ffn_conv_b"][0], NFF),
        pad64(inp["filt_b1"][0]), pad64(inp["filt_freq1"][0]), pad64(inp["filt_b2"][0]), pad64(inp["filt_freq2"][0]),
    ]
    gains = {"g_pre": bc(inp["pre_mix_gain"][0]), "g_q": bc(inp["q_norm_gain"][0]), "g_kv": bc(inp["kv_norm_gain"][0]),
             "g_post": bc(inp["post_mix_gain"][0]), "g_ffn": bc(inp["pre_ffn_gain"][0]), "g_pffn": bc(inp["post_ffn_gain"][0])}
    return np.ascontiguousarray(np.concatenate(parts, axis=1).astype(np.float32)), gains


def kernel(**inputs):
    inp = {k: np.asarray(v) for k, v in inputs.items()}
    if "nc" not in _NC_CACHE:
        _NC_CACHE["nc"] = build_program()
    nc = _NC_CACHE["nc"]
    cst = host_constants()
    f32 = lambda a: np.ascontiguousarray(np.asarray(a, np.float32))
    pack_np, gains_np = _pack_params(inp)
    shared = {
        "w_in": f32(inp["w_in"][0]), "w_uq": f32(inp["w_uq"][0]), "w_ukv": f32(inp["w_ukv"][0]), "w_out": f32(inp["w_out"][0]),
        "w_up": f32(inp["w_up"][0]), "w_down": f32(inp["w_down"][0]),
        "filt_w1": f32(inp["filt_w1"][0]), "filt_w2": f32(inp["filt_w2"][0]), "filt_w3": f32(inp["filt_w3"][0]),
        "pack": pack_np, "hbias": f32(inp["hyena_bias"][0]).reshape(1, HW),
        "dft_fwd": cst["dft_fwd"], "dft_inv": cst["dft_inv"], "window": cst["window"], "zT": cst["zT"],
        "rope_tok": cst["rope_tok"], "rope_feat": cst["rope_feat"], "ident": cst["ident"],
    }
    shared.update(gains_np)
    xs = f32(inp["x"])
    in_maps = []
    for b in range(NCORES):
        m = dict(shared)
        m["x"] = np.ascontiguousarray(xs[b])
        in_maps.append(m)
    res = run_bass_kernel_spmd(nc, in_maps, core_ids=list(range(NCORES)))
    kernel.last = res
    out = np.stack([np.asarray(r["y"], np.float32) for r in res.results], axis=0)
    return out
```

```python
import math
import os
import types
from contextlib import ExitStack

import numpy as np
import ml_dtypes

import concourse.bass as bass
import concourse.mybir as mybir
from concourse.bass_utils import run_bass_kernel_spmd

F32 = mybir.dt.float32
BF16 = mybir.dt.bfloat16
AF = mybir.ActivationFunctionType
ALU = mybir.AluOpType

D = 2048
S = 2048
HW = 1024
NH = 8
DFF = 5632
NFF = DFF // 128
EPS = 1e-6
INW = 3904
PI = math.pi
NCORES = 8

POOL_BYTES = 206 * 1024

DEBUG = os.environ.get("MK_DEBUG", "")


class Buf:
    def __init__(self, name, off=None, nbytes=0):
        self.name = name
        self.off = off
        self.nbytes = nbytes
        self.w = {}
        self.r = {}
        self.dkey = None
        self.excl = False


def _snap(fn):
    if fn.__closure__ is None:
        return fn
    cells = []
    for c in fn.__closure__:
        try:
            cells.append(types.CellType(c.cell_contents))
        except ValueError:
            cells.append(c)
    g = types.FunctionType(fn.__code__, fn.__globals__, fn.__name__, fn.__defaults__, tuple(cells))
    g.__kwdefaults__ = fn.__kwdefaults__
    return g


class Prog:
    ENGS = ("pe", "act", "dve", "pool", "sp")

    def __init__(self, nc, es, pool_ap):
        self.nc = nc
        self.es = es
        self.streams = {e: [] for e in self.ENGS}
        self.semh = {}
        self.cnt = {}
        self.seen = {e: {} for e in self.ENGS}
        self.pool_ap = pool_ap
        self.free_list = [(0, POOL_BYTES)]
        self.ghosts = []
        self.nsem = 0
        self.peak = 0
        self.used = 0
        self.dead = False
        self.dumps = []

    def sem(self, key):
        if key not in self.semh:
            self.semh[key] = self.es.enter_context(self.nc.semaphore("s%d" % self.nsem))
            self.nsem += 1
            self.cnt[key] = 0
        return self.semh[key]

    def alloc(self, name, nbytes, top=False):
        req = nbytes
        nbytes = (nbytes + 63) // 64 * 64
        if self.dead:
            b = Buf(name, 0, nbytes)
            b.req = req
            return b
        order = list(enumerate(self.free_list))
        if top:
            order.reverse()
        for i, (s, e) in order:
            if e - s >= nbytes:
                if top and e - s > nbytes:
                    self.free_list[i] = (s, e - nbytes)
                    s = e - nbytes
                    b = Buf(name, s, nbytes)
                    b.req = req
                else:
                    b = Buf(name, s, nbytes)
                    b.req = req
                    if e - s == nbytes:
                        self.free_list.pop(i)
                    else:
                        self.free_list[i] = (s + nbytes, e)
                keep = []
                for (gs, ge, ev) in self.ghosts:
                    if gs < s + nbytes and ge > s:
                        for k, v in ev.items():
                            b.r[k] = max(b.r.get(k, 0), v)
                        if gs >= s and ge <= s + nbytes:
                            continue
                    keep.append((gs, ge, ev))
                self.ghosts = keep
                self.used += nbytes
                self.peak = max(self.peak, self.used)
                return b
        raise RuntimeError("SBUF pool exhausted allocating %s (%d B), used %d" % (name, nbytes, self.used))

    def free(self, b):
        if self.dead:
            return
        ev = dict(b.r)
        for k, v in b.w.items():
            ev[k] = max(ev.get(k, 0), v)
        self.ghosts.append((b.off, b.off + b.nbytes, ev))
        self.used -= b.nbytes
        fl = self.free_list + [(b.off, b.off + b.nbytes)]
        fl.sort()
        merged = []
        for s, e in fl:
            if merged and merged[-1][1] == s:
                merged[-1] = (merged[-1][0], e)
            else:
                merged.append((s, e))
        self.free_list = merged
        b.off = None

    def view(self, b, dtype=BF16, pattern=None, **kw):
        ap = self.pool_ap[:, b.off // 2:(b.off + b.req) // 2]
        if dtype != BF16:
            ap = ap.bitcast(dtype)
        if pattern is not None:
            ap = ap.rearrange(pattern, **kw)
        return ap

    def _wait(self, eng, deps, raw=None):
        for k, v in deps.items():
            if k == eng and eng == "pe":
                continue
            if k == eng and raw is not None and raw.get(k, 0) < v:
                if raw.get(k, 0) <= self.seen[eng].get(k, 0):
                    continue
                v = raw[k]
            if self.seen[eng].get(k, 0) >= v:
                continue
            self.seen[eng][k] = v
            h = self.sem(k)
            self.streams[eng].append(lambda e, h=h, v=v: e.wait_ge(h, v))

    def _collect(self, reads, writes, parts, eng=None):
        deps = {}

        def add(d):
            for k, v in d.items():
                if v > deps.get(k, 0):
                    deps[k] = v

        for b in reads:
            add(b.w)
            if b.excl:
                add({k: v for k, v in b.r.items() if k != eng})
        self._raw = dict(deps)
        for b in writes:
            add(b.w)
            add(b.r)
        for b in parts:
            add(b.r)
        return deps

    def _commit(self, key, val, reads, writes, parts):
        for b in reads:
            b.r[key] = max(b.r.get(key, 0), val)
        for b in writes:
            b.w = {key: val}
            b.r = {}
        for b in parts:
            b.w[key] = max(b.w.get(key, 0), val)

    def op(self, eng, fn, reads=(), writes=(), parts=(), tick=True):
        if self.dead:
            return
        fn = _snap(fn)
        deps = self._collect(reads, writes, parts, eng)
        self._wait(eng, deps, self._raw)
        h = self.sem(eng)
        if tick:
            self.cnt[eng] += 1
            val = self.cnt[eng]
            self.streams[eng].append(lambda e, fn=fn, h=h: fn(e).then_inc(h, 1))
        else:
            val = self.cnt[eng] + 1
            self.streams[eng].append(lambda e, fn=fn: fn(e))
        self._commit(eng, val, reads, writes, parts)

    def dma(self, q, out, in_, reads=(), writes=(), parts=(), sembuf=None):
        if self.dead:
            return
        deps = self._collect(reads, writes, parts)
        self._wait(q, deps)
        if sembuf.dkey is None:
            sembuf.dkey = ("d", sembuf.name, id(sembuf))
        key = sembuf.dkey
        h = self.sem(key)
        self.cnt[key] += 16
        val = self.cnt[key]
        self.streams[q].append(lambda e, out=out, in_=in_, h=h: e.dma_start(out=out, in_=in_).then_inc(h, 16))
        self._commit(key, val, reads, writes, parts)

    def prewait(self, eng, reads=()):
        if self.dead:
            return
        deps = {}
        for b in reads:
            for k, v in b.w.items():
                deps[k] = max(deps.get(k, 0), v)
        self._wait(eng, deps)

    def wait_all(self, eng, bufs):
        deps = {}
        for b in bufs:
            for d in (b.w, b.r):
                for k, v in d.items():
                    deps[k] = max(deps.get(k, 0), v)
        self._wait(eng, deps)


_CONST_CACHE = {}


def host_constants():
    if _CONST_CACHE:
        return _CONST_CACHE
    bf = ml_dtypes.bfloat16
    L = S
    n = 2 * L
    l = np.arange(L, dtype=np.int64)
    f = np.arange(L, dtype=np.int64)
    m = ((2 * f[None, :] + 1) * l[:, None]) % (2 * n)
    ang = (2.0 * np.pi / (2 * n)) * m.astype(np.float64)
    Cm = np.cos(ang)
    Sm = np.sin(ang)
    def fwd_layout(M):
        return M.reshape(16, 128, 16, 128).transpose(2, 1, 0, 3)
    csf = np.stack([fwd_layout(Cm), fwd_layout(Sm)], axis=2)
    _CONST_CACHE["dft_fwd"] = np.ascontiguousarray(csf.reshape(16, 128, 2 * 16 * 128)).astype(bf)
    def inv_layout(M):
        Mt = (2.0 / n) * M
        return Mt.reshape(8, 256, 16, 128).transpose(0, 3, 2, 1)
    csi = np.stack([inv_layout(Cm), inv_layout(Sm)], axis=2)
    _CONST_CACHE["dft_inv"] = np.ascontiguousarray(csi.reshape(8, 128, 2 * 16 * 256)).astype(bf)
    t = np.linspace(0.0, 1.0, L, dtype=np.float32)
    max_decay = math.log(1e-2) / 0.3
    min_decay = math.log(1e-2) / 1.5
    deltas = np.abs(np.linspace(min_decay, max_decay, HW, dtype=np.float32))
    win = np.exp(-t[:, None].astype(np.float64) * deltas[None, :].astype(np.float64)) + 0.05
    _CONST_CACHE["window"] = np.ascontiguousarray(win.astype(np.float32))
    bands = 16
    wv = (np.float32(2.0 * math.pi) * np.arange(L, dtype=np.float32) / np.float32(L)).astype(np.float32)
    fr = np.linspace(1e-4, bands - 1, bands, dtype=np.float32)
    a2 = (wv[:, None] * fr[None, :]).astype(np.float32).astype(np.float64)
    z = np.concatenate([t[:, None].astype(np.float64), np.cos(a2), -np.sin(a2)], axis=1)
    _CONST_CACHE["zT"] = np.ascontiguousarray(z.T.astype(np.float32))
    pos = np.arange(S, dtype=np.float32)
    inv_freq = (1.0 / (np.float32(10000.0) ** (np.arange(0, 64, 2, dtype=np.float32) / np.float32(64)))).astype(np.float32)
    ra = (pos[:, None] * inv_freq[None, :]).astype(np.float32)
    ra = np.concatenate([ra, ra], axis=1).astype(np.float64)
    cos = np.cos(ra)
    sin = np.sin(ra)
    sgn = np.concatenate([-np.ones(32), np.ones(32)])[None, :]
    sin_s = sin * sgn
    tok = np.stack([cos, sin_s], axis=1)
    tok = tok.reshape(16, 128, 2, 64).transpose(1, 0, 2, 3)
    _CONST_CACHE["rope_tok"] = np.ascontiguousarray(tok.reshape(128, 16 * 2 * 64)).astype(np.float32)
    scale = (128 + 64) ** -0.5
    ft = np.stack([cos.T * scale, sin_s.T * scale], axis=1)
    _CONST_CACHE["rope_feat"] = np.ascontiguousarray(ft.reshape(64, 2 * S)).astype(bf)
    _CONST_CACHE["ident"] = np.eye(128, dtype=np.float32).astype(bf)
    return _CONST_CACHE


def build_program(stop_after=None):
    nc = bass.Bass("TRN2", target_bir_lowering=False)

    def din(name, shape, dt=F32):
        return nc.dram_tensor(name, list(shape), dt, kind="ExternalInput").ap()

    x = din("x", [S, D])
    w_in = din("w_in", [D, INW])
    w_uq = din("w_uq", [512, 1536])
    w_ukv = din("w_ukv", [256, 2048])
    w_out = din("w_out", [D, D])
    w_up = din("w_up", [D, 2 * DFF])
    w_down = din("w_down", [DFF, D])
    filt_w1 = din("filt_w1", [33, 64])
    filt_w2 = din("filt_w2", [64, 64])
    filt_w3 = din("filt_w3", [64, 2048])
    pk_names = [
        ("hconv_w", 72), ("hconv_b", 24), ("hgain", 8), ("again", 8), ("fconv_w", 132), ("fconv_b", 44),
        ("fb1", 1), ("ffr1", 1), ("fb2", 1), ("ffr2", 1),
    ]
    pk_off = {}
    o = 0
    for nm, sz in pk_names:
        pk_off[nm] = (o, sz)
        o += sz
    NPK = o
    pack = din("pack", [128, NPK])
    gains_d = {nm: din(nm, [128, sz]) for nm, sz in (("g_pre", 2048), ("g_q", 512), ("g_kv", 256), ("g_post", 2048), ("g_ffn", 2048), ("g_pffn", 2048))}
    hbias = din("hbias", [1, HW])
    dft_fwd = din("dft_fwd", [16, 128, 2 * 16 * 128], BF16)
    dft_inv = din("dft_inv", [8, 128, 2 * 16 * 256], BF16)
    window = din("window", [S, HW])
    zT = din("zT", [33, S])
    rope_tok = din("rope_tok", [128, 16 * 2 * 64])
    rope_feat = din("rope_feat", [64, 2 * S], BF16)
    ident_d = din("ident", [128, 128], BF16)

    y = nc.dram_tensor("y", [S, D], F32, kind="ExternalOutput").ap()
    hn_scr = nc.dram_tensor("hn_scr", [128, 16, S], BF16, kind="Internal").ap()
    dbg = {}
    if DEBUG:
        for nm, shp, dt in [("d_xnT", [128, 16 * S], BF16), ("d_x0T", [128, 8 * S], BF16), ("d_utok", [128, 2 * 16 * 512], BF16),
                            ("d_cq", [128, 4 * S], BF16), ("d_ckv", [128, 2 * S], BF16), ("d_kpe", [128, S], BF16),
                            ("d_yh", [128, 8 * S], BF16), ("d_ya", [128, 8 * S], BF16), ("d_A", [128, 16 * 512], BF16)]:
            dbg[nm] = nc.dram_tensor(nm, shp, dt, kind="ExternalOutput").ap()

    with ExitStack() as es:
        pool_t = es.enter_context(nc.sbuf_tensor("pool", [128, POOL_BYTES // 2], BF16))
        psum_t = es.enter_context(nc.psum_tensor("psum", [128, 4096], F32))
        P = Prog(nc, es, pool_t[:, :])
        ps_f = psum_t[:, :]
        ps_b = psum_t[:, :].bitcast(BF16)
        banks = [Buf("bank%d" % i) for i in range(8)]
        for b_ in banks:
            b_.excl = True

        def pf(b, a=0, n=512, rows=128):
            return ps_f[0:rows, b * 512 + a: b * 512 + a + n]

        def pb(b, a=0, n=1024, rows=128):
            return ps_b[0:rows, b * 1024 + a: b * 1024 + a + n]

        cbuf = P.alloc("pack", NPK * 4)
        packv = P.view(cbuf, F32)

        def pkc(nm, a=0, n=None):
            o_, sz = pk_off[nm]
            if n is None:
                n = sz - a
            return packv[:, o_ + a:o_ + a + n]

        P.dma("sp", packv, pack[:, :], writes=[cbuf], sembuf=cbuf)

        def load_gain(nm):
            sz = gains_d[nm].shape[1]
            gb = P.alloc(nm, sz * 4)
            gv = P.view(gb, F32)
            P.dma("sp", gv, gains_d[nm][:, :], writes=[gb], sembuf=gb)
            return gb, gv
        identb = P.alloc("ident", 128 * 2)
        ident = P.view(identb)
        P.dma("sp", ident, ident_d[:, :], writes=[identb], sembuf=identb)
        onesb = P.alloc("ones", 128 * 2)
        ones = P.view(onesb)
        P.op("dve", lambda e: e.memset(ones, 1.0), writes=[onesb])
        smallb = P.alloc("small", 64 * 4)
        small = P.view(smallb, F32)
        junkb = P.alloc("junk", 2048 * 2)
        junk = P.view(junkb)

        sm_idx = [0]

        def rstd_from_ss(ss_ap, inv_n, ss_buf):
            i = sm_idx[0] % 60
            sm_idx[0] += 1
            col = small[:, i:i + 1]
            P.op("act", lambda e: e.activation(col, ss_ap, AF.Sqrt, bias=pk_eps, scale=inv_n), reads=[ss_buf, epsb], parts=[smallb])
            P.op("dve", lambda e: e.reciprocal(col, col), reads=[smallb], parts=[smallb])
            return col

        epsb = P.alloc("eps", 4 * 4)
        pk_eps = P.view(epsb, F32)[:, 0:1]
        P.op("dve", lambda e: e.memset(pk_eps, EPS), writes=[epsb])
        ssb = P.alloc("ss", 64 * 4)
        ssv = P.view(ssb, F32)
        ss_i = [0]

        def ss_col():
            i = ss_i[0] % 64
            ss_i[0] += 1
            return ssv[:, i:i + 1]

        wsl = [P.alloc("wsl%d" % i, 16 * 128 * 2, top=True) for i in range(3)]
        gpre_b, gpre = load_gain("g_pre")
        xnTb = P.alloc("xnT", 16 * S * 2)
        xnT = P.view(xnTb, BF16, "p (c t) -> p c t", t=S)
        xts = [P.alloc("xt%d" % i, D * 4) for i in range(4)]
        xss = [P.alloc("xs%d" % i, D * 2) for i in range(3)]
        def p1_a(i):
            xt_b = xts[i % 4]
            xs_b = xss[i % 3]
            xt = P.view(xt_b, F32)
            xs = P.view(xs_b)
            P.dma("sp", xt, x[i * 128:(i + 1) * 128, :], writes=[xt_b], sembuf=xt_b)
            ssc = ss_col()
            P.op("act", lambda e, xt=xt, ssc=ssc: e.activation(junk, xt, AF.Square, accum_out=ssc), reads=[xt_b], parts=[ssb])
            rs = rstd_from_ss(ssc, 1.0 / D, ssb)
            P.op("dve", lambda e, xt=xt, xs=xs, rs=rs: e.scalar_tensor_tensor(xs, xt, rs, gpre, op0=ALU.mult, op1=ALU.mult),
                 reads=[xt_b, smallb, gpre_b], writes=[xs_b])

        def p1_b(i):
            xs_b = xss[i % 3]
            xs = P.view(xs_b)
            for hb in range(2):
                bk = (2 * i + hb) % 4
                for c in range(8):
                    cc = hb * 8 + c
                    P.op("pe", lambda e, bk=bk, c=c, cc=cc, xs=xs: e.transpose(pb(bk, c * 128, 128), xs[:, cc * 128:(cc + 1) * 128], ident),
                         reads=[xs_b, identb], parts=[banks[bk]] if c else (), writes=[banks[bk]] if c == 0 else (), tick=(c == 7))
                eng = "act" if hb == 0 else "dve"
                dst = xnT[:, hb * 8:(hb + 1) * 8, i * 128:(i + 1) * 128]
                src = pb(bk).rearrange("p (c t) -> p c t", t=128)
                if eng == "act":
                    P.op("act", lambda e, dst=dst, src=src: e.copy(dst, src), reads=[banks[bk]], parts=[xnTb])
                else:
                    P.op("dve", lambda e, dst=dst, src=src: e.tensor_copy(dst, src), reads=[banks[bk]], parts=[xnTb])

        p1_a(0)
        for i in range(16):
            if i + 1 < 16:
                p1_a(i + 1)
            p1_b(i)
        for b in xts + xss + [gpre_b]:
            P.free(b)

        def dump(name, buf, ap2d, col0=0):
            if name in dbg:
                n = ap2d.shape[1]
                for a in range(0, n, 4096):
                    b_ = min(n, a + 4096)
                    P.dma("sp", dbg[name][:, col0 + a:col0 + b_], ap2d[:, a:b_], reads=[buf], sembuf=buf)
                if not P.dead:
                    P.dumps.append(buf)

        def checkpoint(k):
            if stop_after is not None and k >= stop_after and not P.dead:
                P.wait_all("sp", P.dumps)
                P.dead = True

        dump("d_xnT", xnTb, P.view(xnTb))
        checkpoint(1)

        x0Tb = P.alloc("x0T", 8 * S * 2)
        x0T = P.view(x0Tb, BF16, "p (c t) -> p c t", t=S)
        utokb = [P.alloc("utok%d" % h, 16 * 512 * 2) for h in range(2)]
        utok = [P.view(b, BF16, "p (s c) -> p s c", c=512) for b in utokb]
        cx1b = P.alloc("cx1", S * 4)
        cob = P.alloc("co", S * 4)
        ubb = P.alloc("ub", S * 2)
        cx1 = P.view(cx1b, F32)
        co = P.view(cob, F32)
        ub = P.view(ubb)
        widx = [0]

        def hy_chunk(col0, setk, part_idx, j):
            wb_ = wsl[widx[0] % 3]
            widx[0] += 1
            wv_ = P.view(wb_, BF16, "p (c n) -> p c n", n=128)
            P.dma("pool", wv_, w_in[:, col0:col0 + 128].rearrange("(c p) n -> p c n", p=128), writes=[wb_], sembuf=wb_)
            for tg in range(4):
                bk = setk * 4 + tg
                for c in range(16):
                    P.op("pe", lambda e, bk=bk, c=c, tg=tg, wv_=wv_: e.matmul(pf(bk), lhsT=wv_[:, c, :], rhs=xnT[:, c, tg * 512:(tg + 1) * 512],
                                                                      start=(c == 0), stop=(c == 15)),
                         reads=[wb_, xnTb], writes=[banks[bk]] if c == 0 else (), parts=[banks[bk]] if c else (), tick=(c == 15))

        def conv_chunk(setk, chunk24, dst, dst_buf):
            raw = ps_f[:, setk * 2048:(setk + 1) * 2048]
            bset = banks[setk * 4:(setk + 1) * 4]
            w0 = pkc("hconv_w", 0 * 24 + chunk24, 1)
            w1 = pkc("hconv_w", 1 * 24 + chunk24, 1)
            w2 = pkc("hconv_w", 2 * 24 + chunk24, 1)
            bb = pkc("hconv_b", chunk24, 1)
            P.op("act", lambda e: e.activation(dst, raw, AF.Identity, bias=bb, scale=w1), reads=bset + [cbuf], writes=[dst_buf])
            P.op("dve", lambda e: e.scalar_tensor_tensor(dst[:, 1:S], raw[:, 0:S - 1], w0, dst[:, 1:S], op0=ALU.mult, op1=ALU.add),
                 reads=bset + [cbuf], writes=[dst_buf])
            P.op("dve", lambda e: e.scalar_tensor_tensor(dst[:, 0:S - 1], raw[:, 1:S], w2, dst[:, 0:S - 1], op0=ALU.mult, op1=ALU.add),
                 reads=bset + [cbuf], writes=[dst_buf])

        for j in range(8):
            sa, sb_ = (0, 1) if j % 2 == 0 else (1, 0)
            hy_chunk(HW + j * 128, sa, 1, j)
            hy_chunk(2 * HW + j * 128, sb_, 2, j)
            conv_chunk(sa, 8 + j, cx1, cx1b)
            hy_chunk(j * 128, sa, 0, j)
            conv_chunk(sb_, 16 + j, co, cob)
            P.op("dve", lambda e: e.tensor_tensor(ub, cx1, co, op=ALU.mult), reads=[cx1b, cob], writes=[ubb])
            hf = j // 4
            jj = j % 4
            for hb in range(2):
                bk = sb_ * 4 + hb
                for c in range(8):
                    sc = hb * 8 + c
                    P.op("pe", lambda e, bk=bk, c=c, sc=sc: e.transpose(pb(bk, c * 128, 128), ub[:, sc * 128:(sc + 1) * 128], ident),
                         reads=[ubb, identb], parts=[banks[bk]] if c else (), writes=[banks[bk]] if c == 0 else (), tick=(c == 7))
                dst = utok[hf][:, hb * 8:(hb + 1) * 8, jj * 128:(jj + 1) * 128]
                src = pb(bk).rearrange("p (c t) -> p c t", t=128)
                P.op("act", lambda e, dst=dst, src=src: e.copy(dst, src), reads=[banks[bk]], parts=[utokb[hf]])
            conv_chunk(sa, j, co, cob)
            P.op("act", lambda e, j=j: e.copy(x0T[:, j, :], co), reads=[cob], parts=[x0Tb])
        for b in wsl + [cx1b, cob, ubb]:
            P.free(b)
        dump("d_x0T", x0Tb, P.view(x0Tb))
        if "d_utok" in dbg:
            dump("d_utok", utokb[0], P.view(utokb[0]), 0)
            dump("d_utok", utokb[1], P.view(utokb[1]), 8192)
        checkpoint(2)

        cqb = P.alloc("cq_nT", 4 * S * 2)
        ckvb = P.alloc("ckv_nT", 2 * S * 2)
        kpeb = P.alloc("k_peT", S * 2)
        cq_nT = P.view(cqb, BF16, "p (c t) -> p c t", t=S)
        ckv_nT = P.view(ckvb, BF16, "p (c t) -> p c t", t=S)
        k_peT = P.view(kpeb)
        wmb = P.alloc("wm", 16 * 832 * 2)
        wm = P.view(wmb, BF16, "p (c n) -> p c n", n=832)
        for c4 in range(4):
            P.dma("pool", wm[:, c4 * 4:(c4 + 1) * 4, :], w_in[c4 * 512:(c4 + 1) * 512, 3072:3904].rearrange("(c p) n -> p c n", p=128),
                  parts=[wmb], sembuf=wmb)
        rtb = P.alloc("rope_tok", 16 * 2 * 64 * 4)
        rt = P.view(rtb, F32, "p (i k d) -> p i k d", k=2, d=64)
        P.dma("sp", P.view(rtb, F32), rope_tok[:, :], writes=[rtb], sembuf=rtb)
        gq_b, gq = load_gain("g_q")
        gkv_b, gkv = load_gain("g_kv")
        latb = [P.alloc("lat%d" % i, 896 * 2) for i in range(2)]
        t1b = P.alloc("t1", 64 * 4)
        t2b = P.alloc("t2", 64 * 4)
        t1 = P.view(t1b, F32)
        t2 = P.view(t2b, F32)
        def p2a_m(i):
                bA, bB, bT = (0, 1, 2) if i % 2 == 0 else (3, 4, 5)
                tok = slice(i * 128, (i + 1) * 128)
                for c in range(16):
                    P.op("pe", lambda e, c=c, tok=tok, bA=bA: e.matmul(pf(bA), lhsT=xnT[:, c, tok], rhs=wm[:, c, 0:512], start=(c == 0), stop=(c == 15)),
                         reads=[xnTb, wmb], writes=[banks[bA]] if c == 0 else (), parts=[banks[bA]] if c else (), tick=(c == 15))
                for c in range(16):
                    P.op("pe", lambda e, c=c, tok=tok, bB=bB: e.matmul(pf(bB, 0, 320), lhsT=xnT[:, c, tok], rhs=wm[:, c, 512:832], start=(c == 0), stop=(c == 15)),
                         reads=[xnTb, wmb], writes=[banks[bB]] if c == 0 else (), parts=[banks[bB]] if c else (), tick=(c == 15))

        def p2a_e(i):
                bA, bB, bT = (0, 1, 2) if i % 2 == 0 else (3, 4, 5)
                tok = slice(i * 128, (i + 1) * 128)
                lb = latb[i % 2]
                lat = P.view(lb)
                s1 = ss_col()
                P.op("act", lambda e, s1=s1, bA=bA: e.activation(junk[:, 0:512], pf(bA), AF.Square, accum_out=s1), reads=[banks[bA]], parts=[ssb])
                r1 = rstd_from_ss(s1, 1.0 / 512, ssb)
                P.op("dve", lambda e, lat=lat, r1=r1, bA=bA: e.scalar_tensor_tensor(lat[:, 0:512], pf(bA), r1, gq, op0=ALU.mult, op1=ALU.mult),
                     reads=[banks[bA], smallb, gq_b], parts=[lb])
                s2 = ss_col()
                P.op("act", lambda e, s2=s2, bB=bB: e.activation(junk[:, 0:256], pf(bB, 0, 256), AF.Square, accum_out=s2), reads=[banks[bB]], parts=[ssb])
                r2 = rstd_from_ss(s2, 1.0 / 256, ssb)
                P.op("dve", lambda e, lat=lat, r2=r2, bB=bB: e.scalar_tensor_tensor(lat[:, 512:768], pf(bB, 0, 256), r2, gkv, op0=ALU.mult, op1=ALU.mult),
                     reads=[banks[bB], smallb, gkv_b], parts=[lb])
                P.op("dve", lambda e, i=i, bB=bB: e.tensor_tensor(t1, pf(bB, 256, 64), rt[:, i, 0, :], op=ALU.mult), reads=[banks[bB], rtb], writes=[t1b])
                P.op("dve", lambda e, i=i, bB=bB: e.tensor_tensor(t2[:, 0:32], pf(bB, 288, 32), rt[:, i, 1, 0:32], op=ALU.mult), reads=[banks[bB], rtb], writes=[t2b])
                P.op("dve", lambda e, i=i, bB=bB: e.tensor_tensor(t2[:, 32:64], pf(bB, 256, 32), rt[:, i, 1, 32:64], op=ALU.mult), reads=[banks[bB], rtb], parts=[t2b])
                P.op("dve", lambda e, lat=lat: e.tensor_tensor(lat[:, 768:832], t1, t2, op=ALU.add), reads=[t1b, t2b], parts=[lb])
                P.op("dve", lambda e, lat=lat: e.tensor_tensor(lat[:, 832:896], t1, t2, op=ALU.add), reads=[t1b, t2b], parts=[lb])

        def p2a_t(i):
                bA, bB, bT = (0, 1, 2) if i % 2 == 0 else (3, 4, 5)
                tok = slice(i * 128, (i + 1) * 128)
                lb = latb[i % 2]
                lat = P.view(lb)
                for c in range(7):
                    P.op("pe", lambda e, c=c, lat=lat, bT=bT: e.transpose(pb(bT, c * 128, 128), lat[:, c * 128:(c + 1) * 128], ident),
                         reads=[lb, identb], writes=[banks[bT]] if c == 0 else (), parts=[banks[bT]] if c else (), tick=(c == 6))
                P.op("act", lambda e, tok=tok, bT=bT: e.copy(cq_nT[:, :, tok], pb(bT, 0, 512).rearrange("p (c t) -> p c t", t=128)), reads=[banks[bT]], parts=[cqb])
                P.op("act", lambda e, tok=tok, bT=bT: e.copy(ckv_nT[:, :, tok], pb(bT, 512, 256).rearrange("p (c t) -> p c t", t=128)), reads=[banks[bT]], parts=[ckvb])
                P.op("act", lambda e, tok=tok, bT=bT: e.copy(k_peT[:, tok], pb(bT, 768, 128)), reads=[banks[bT]], parts=[kpeb])

        p2a_m(0)
        p2a_e(0)
        for i in range(16):
            if i + 1 < 16:
                p2a_m(i + 1)
                p2a_e(i + 1)
            p2a_t(i)
        for b in [wmb, rtb, t1b, t2b, gq_b, gkv_b] + latb:
            P.free(b)
        P.free(xnTb)
        dump("d_cq", cqb, P.view(cqb))
        dump("d_ckv", ckvb, P.view(ckvb))
        dump("d_kpe", kpeb, P.view(kpeb))
        checkpoint(3)

        P.free(junkb)
        zTb = P.alloc("zT", S * 4)
        h1b = P.alloc("h1T", S * 4)
        h2b = P.alloc("h2T", S * 2)
        w3b = P.alloc("w3", 2048 * 2)
        fwb = P.alloc("fw12", 128 * 4)
        zTv = P.view(zTb, F32)
        h1T = P.view(h1b, F32)
        h2T = P.view(h2b)
        w3v = P.view(w3b)
        fw = P.view(fwb, F32)
        P.dma("sp", zTv[0:33, :], zT[:, :], writes=[zTb], sembuf=zTb)
        P.dma("pool", w3v[0:64, :], filt_w3[:, :], writes=[w3b], sembuf=w3b)
        P.dma("sp", fw[0:33, 0:64], filt_w1[:, :], parts=[fwb], sembuf=fwb)
        P.dma("sp", fw[0:64, 64:128], filt_w2[:, :], parts=[fwb], sembuf=fwb)
        argb = P.alloc("arg", S * 4)
        wrb = P.alloc("wrap", S * 4)
        arg = P.view(argb, F32)
        wr = P.view(wrb, F32)

        def sin_layer(lhsT, krows, rhs_ap, rhs_buf, dstT, dst_buf, bname, frname):
            for tg in range(4):
                P.op("pe", lambda e, tg=tg: e.matmul(pf(tg, 0, 512, 64), lhsT=lhsT, rhs=rhs_ap[0:krows, tg * 512:(tg + 1) * 512], start=True, stop=True),
                     reads=[fwb, rhs_buf], writes=[banks[tg]])
            raw = ps_f[0:64, 0:2048]
            P.op("dve", lambda e: e.tensor_scalar(arg[0:64, :], raw, pkc(bname)[0:64, :], pkc(frname)[0:64, :], op0=ALU.add, op1=ALU.mult),
                 reads=banks[0:4] + [cbuf], writes=[argb])
            P.op("dve", lambda e: e.tensor_scalar(wr[0:64, :], arg[0:64, :], PI, 2 * PI, op0=ALU.is_gt, op1=ALU.mult), reads=[argb], writes=[wrb])
            P.op("dve", lambda e: e.tensor_tensor(arg[0:64, :], arg[0:64, :], wr[0:64, :], op=ALU.subtract), reads=[wrb, argb], writes=[argb])
            P.op("dve", lambda e: e.tensor_scalar(wr[0:64, :], arg[0:64, :], -PI, 2 * PI, op0=ALU.is_lt, op1=ALU.mult), reads=[argb], writes=[wrb])
            P.op("dve", lambda e: e.tensor_tensor(arg[0:64, :], arg[0:64, :], wr[0:64, :], op=ALU.add), reads=[wrb, argb], writes=[argb])
            P.op("act", lambda e: e.activation(dstT[0:64, :], arg[0:64, :], AF.Sin), reads=[argb], writes=[dst_buf])

        sin_layer(fw[0:33, 0:64], 33, zTv, zTb, h1T, h1b, "fb1", "ffr1")
        sin_layer(fw[0:64, 64:128], 64, h1T, h1b, h2T, h2b, "fb2", "ffr2")
        for b in [zTb, h1b, argb, wrb]:
            P.free(b)
        hbb = P.alloc("hbias", 512 * 4)
        hbv = P.view(hbb, F32)

        yhb = [None, None]
        ssh = {}
        for hf in range(2):
            c0 = hf * 512
            P.dma("sp", hbv[0:1, :], hbias[:, c0:c0 + 512], writes=[hbb], sembuf=hbb)
            Ab = P.alloc("A", 16 * 512 * 2)
            Bb = P.alloc("Bm", 16 * 512 * 2)
            Av = P.view(Ab, BF16, "p (l c) -> p l c", c=512)
            Bv = P.view(Bb, BF16, "p (l c) -> p l c", c=512)
            winb = [P.alloc("win%d" % i, 512 * 4) for i in range(2)]
            tFb = P.alloc("tF", 512 * 4)
            tBb = P.alloc("tB", 512 * 4)
            tF = P.view(tFb, F32)
            tB = P.view(tBb, F32)
            for lc in range(16):
                wb_ = winb[lc % 2]
                wv_ = P.view(wb_, F32)
                P.dma("sp", wv_, window[lc * 128:(lc + 1) * 128, c0:c0 + 512], writes=[wb_], sembuf=wb_)
                bF, bB = (0, 1) if lc % 2 == 0 else (2, 3)
                P.op("pe", lambda e, lc=lc, bF=bF: e.matmul(pf(bF), lhsT=h2T[0:64, lc * 128:(lc + 1) * 128], rhs=w3v[0:64, c0:c0 + 512], start=True, stop=True),
                     reads=[h2b, w3b], writes=[banks[bF]])
                P.op("pe", lambda e, lc=lc, bB=bB: e.matmul(pf(bB), lhsT=h2T[0:64, lc * 128:(lc + 1) * 128], rhs=w3v[0:64, HW + c0:HW + c0 + 512], start=True, stop=True),
                     reads=[h2b, w3b], writes=[banks[bB]])
                P.op("dve", lambda e, wv_=wv_, bF=bF: e.tensor_tensor(tF, pf(bF), wv_, op=ALU.mult), reads=[banks[bF], wb_], writes=[tFb])
                P.op("dve", lambda e, wv_=wv_, bB=bB: e.tensor_tensor(tB, pf(bB), wv_, op=ALU.mult), reads=[banks[bB], wb_], writes=[tBb])
                P.op("dve", lambda e, lc=lc: e.tensor_tensor(Av[:, lc, :], tF, tB, op=ALU.add), reads=[tFb, tBb], parts=[Ab])
                P.op("dve", lambda e, lc=lc: e.tensor_tensor(Bv[:, lc, :], tB, tF, op=ALU.subtract), reads=[tFb, tBb], parts=[Bb])
                if lc == 0:
                    P.op("dve", lambda e: e.tensor_tensor(Av[0:1, 0, :], tF[0:1, :], hbv[0:1, :], op=ALU.add), reads=[tFb, hbb, Ab], parts=[Ab])
            for b in winb + [tFb, tBb]:
                P.free(b)
            if hf == 0:
                dump("d_A", Ab, P.view(Ab))
                if stop_after == 3.5:
                    checkpoint(3.5)
            Yb = P.alloc("Yre", 16 * 512 * 2)
            Zb = P.alloc("Z", 16 * 512 * 2)
            Yv = P.view(Yb, BF16, "p (f c) -> p f c", c=512)
            Zv = P.view(Zb, BF16, "p (f c) -> p f c", c=512)
            csb = [P.alloc("cs%d" % i, 2 * 16 * 128 * 2) for i in range(2)]
            kreb = P.alloc("kre", 512 * 4)
            kimb = P.alloc("kim", 512 * 4)
            pb_ = [P.alloc("p%d" % i, 512 * 4) for i in range(2)]
            kre = P.view(kreb, F32)
            kim = P.view(kimb, F32)
            pv = [P.view(b, F32) for b in pb_]
            uv = utok[hf]
            for fc in range(16):
                cb = csb[fc % 2]
                cv = P.view(cb, BF16, "p (k l j) -> p k l j", k=2, j=128)
                P.dma("sp", P.view(cb), dft_fwd[fc, :, :], writes=[cb], sembuf=cb)
                bs = 0 if fc % 2 == 0 else 4
                bKre, bKim, bUre, bUs = bs, bs + 1, bs + 2, bs + 3
                for lc in range(16):
                    st_, sp_ = (lc == 0), (lc == 15)
                    for (bk, k, rhs, rb) in ((bKre, 0, Av[:, lc, :], Ab), (bUre, 0, uv[:, lc, :], utokb[hf]), (bKim, 1, Bv[:, lc, :], Bb), (bUs, 1, uv[:, lc, :], utokb[hf])):
                        P.op("pe", lambda e, bk=bk, k=k, lc=lc, rhs=rhs, st_=st_, sp_=sp_, cv=cv: e.matmul(pf(bk), lhsT=cv[:, k, lc, :], rhs=rhs, start=st_, stop=sp_),
                             reads=[cb, rb], writes=[banks[bk]] if st_ else (), parts=() if st_ else [banks[bk]], tick=sp_)
                P.op("act", lambda e, bKre=bKre: e.copy(kre, pf(bKre)), reads=[banks[bKre]], writes=[kreb])
                P.op("act", lambda e, bKim=bKim: e.copy(kim, pf(bKim)), reads=[banks[bKim]], writes=[kimb])
                P.op("dve", lambda e, bUre=bUre: e.tensor_tensor(pv[0], pf(bUre), kre, op=ALU.mult), reads=[banks[bUre], kreb], writes=[pb_[0]])
                P.op("dve", lambda e, bUs=bUs: e.tensor_tensor(pv[1], pf(bUs), kim, op=ALU.mult), reads=[banks[bUs], kimb], writes=[pb_[1]])
                P.op("dve", lambda e, fc=fc: e.tensor_tensor(Yv[:, fc, :], pv[0], pv[1], op=ALU.add), reads=[pb_[0], pb_[1]], parts=[Yb])
                P.op("dve", lambda e, bUs=bUs: e.tensor_tensor(pv[0], pf(bUs), kre, op=ALU.mult), reads=[banks[bUs], kreb], writes=[pb_[0]])
                P.op("dve", lambda e, bUre=bUre: e.tensor_tensor(pv[1], pf(bUre), kim, op=ALU.mult), reads=[banks[bUre], kimb], writes=[pb_[1]])
                P.op("dve", lambda e, fc=fc: e.tensor_tensor(Zv[:, fc, :], pv[0], pv[1], op=ALU.subtract), reads=[pb_[0], pb_[1]], parts=[Zb])
            for b in csb + [kreb, kimb] + pb_ + [Ab, Bb]:
                P.free(b)
            P.free(utokb[hf])
            if hf == 0:
                ssh["b"] = P.alloc("ss_h", S * 4)
            ssh_b = ssh["b"]
            ss_h = P.view(ssh_b, F32)
            yhb[hf] = P.alloc("yhT%d" % hf, 4 * S * 2)
            yhv = P.view(yhb[hf], BF16, "p (c t) -> p c t", t=S)
            ctb = [P.alloc("ct%d" % i, 2 * 16 * 256 * 2) for i in range(2)]
            yxb = [P.alloc("yx%d" % i, 256 * 4) for i in range(2)]
            sqb = [P.alloc("sq%d" % i, 256 * 2) for i in range(2)]
            it = 0
            pend_inv = []
            for tg in range(8):
                tb = ctb[tg % 2]
                tv = P.view(tb, BF16, "p (k f j) -> p k f j", k=2, j=256)
                P.dma("sp", P.view(tb), dft_inv[tg, :, :], writes=[tb], sembuf=tb)
                tsl = slice(tg * 256, (tg + 1) * 256)
                bS = 6 + (tg % 2)
                for cc in range(4):
                    bk = it % 4
                    yx_b = yxb[it % 2]
                    sq_b = sqb[it % 2]
                    yx = P.view(yx_b, F32)
                    sq = P.view(sq_b)
                    it += 1
                    for fc in range(16):
                        P.op("pe", lambda e, bk=bk, fc=fc, cc=cc, tv=tv: e.matmul(pf(bk, 0, 256), lhsT=Yv[:, fc, cc * 128:(cc + 1) * 128], rhs=tv[:, 0, fc, :], start=(fc == 0), stop=False),
                             reads=[Yb, tb], writes=[banks[bk]] if fc == 0 else (), parts=[banks[bk]] if fc else (), tick=False)
                        P.op("pe", lambda e, bk=bk, fc=fc, cc=cc, tv=tv: e.matmul(pf(bk, 0, 256), lhsT=Zv[:, fc, cc * 128:(cc + 1) * 128], rhs=tv[:, 1, fc, :], start=False, stop=(fc == 15)),
                             reads=[Zb, tb], parts=[banks[bk]], tick=(fc == 15))
                    ch = hf * 4 + cc
                    for f_ in pend_inv:
                        f_()
                    pend_inv.clear()
                    P.op("dve", lambda e, bk=bk, ch=ch, tsl=tsl, yx=yx: e.tensor_tensor(yx, pf(bk, 0, 256), x0T[:, ch, tsl], op=ALU.mult), reads=[banks[bk], x0Tb], writes=[yx_b])
                    P.op("act", lambda e, yx=yx, sq=sq: e.activation(sq, yx, AF.Square), reads=[yx_b], writes=[sq_b])
                    P.op("act", lambda e, yx=yx, cc=cc, tsl=tsl, ch=ch, yhv=yhv: e.activation(yhv[:, cc, tsl], yx, AF.Copy, scale=pkc("hgain", ch, 1)), reads=[yx_b, cbuf], parts=[yhb[hf]])

                    def ones_mm(sq=sq, sq_b=sq_b, bS=bS, cc=cc, tsl=tsl, hf=hf):
                        P.op("pe", lambda e: e.matmul(pf(bS, 0, 256), lhsT=ones, rhs=sq, start=(cc == 0), stop=(cc == 3)),
                             reads=[sq_b, onesb], writes=[banks[bS]] if cc == 0 else (), parts=[banks[bS]] if cc else ())
                        if cc == 3:
                            if hf == 0:
                                P.op("dve", lambda e: e.tensor_copy(ss_h[:, tsl], pf(bS, 0, 256)), reads=[banks[bS]], parts=[ssh_b])
                            else:
                                P.op("dve", lambda e: e.tensor_tensor(ss_h[:, tsl], ss_h[:, tsl], pf(bS, 0, 256), op=ALU.add), reads=[banks[bS], ssh_b], parts=[ssh_b])
                    pend_inv.append(ones_mm)
            for f_ in pend_inv:
                f_()
            pend_inv.clear()
            for b in ctb + yxb + sqb + [Yb, Zb]:
                P.free(b)
        for b in [h2b, w3b, fwb, hbb, x0Tb]:
            P.free(b)
        P.op("act", lambda e: e.activation(ss_h, ss_h, AF.Sqrt, bias=pk_eps, scale=1.0 / HW), reads=[ssh_b, epsb], writes=[ssh_b])
        P.op("dve", lambda e: e.reciprocal(ss_h, ss_h), reads=[ssh_b], writes=[ssh_b])
        for hf in range(2):
            yhv = P.view(yhb[hf], BF16, "p (c t) -> p c t", t=S)
            for cc in range(4):
                P.op("dve", lambda e, yhv=yhv, cc=cc: e.tensor_tensor(yhv[:, cc, :], yhv[:, cc, :], ss_h, op=ALU.mult), reads=[ssh_b, yhb[hf]], writes=[yhb[hf]])
        P.free(ssh_b)
        if "d_yh" in dbg:
            for hf in range(2):
                dump("d_yh", yhb[hf], P.view(yhb[hf]), hf * 4 * S)
        checkpoint(4)

        Vb = P.alloc("V", 16 * 1024 * 2)
        Vv = P.view(Vb, BF16, "p (k c) -> p k c", c=1024)
        wvb = P.alloc("wv", 2 * 1024 * 2)
        wvv = P.view(wvb, BF16, "p (c h d) -> p c h d", c=2, d=128)
        wkv_r = w_ukv.rearrange("(c p) (h t) -> p c h t", p=128, t=256)
        for c in range(2):
            P.dma("pool", wvv[:, c, :, :], wkv_r[:, c, :, 128:256], parts=[wvb], sembuf=wvb)
        wvf = P.view(wvb, BF16, "p (c n) -> p c n", c=2)
        for kc in range(16):
            for hh in range(2):
                bk = (kc * 2 + hh) % 4
                for c in range(2):
                    P.op("pe", lambda e, bk=bk, kc=kc, c=c, hh=hh: e.matmul(pf(bk), lhsT=ckv_nT[:, c, kc * 128:(kc + 1) * 128], rhs=wvf[:, c, hh * 512:(hh + 1) * 512], start=(c == 0), stop=(c == 1)),
                         reads=[ckvb, wvb], writes=[banks[bk]] if c == 0 else (), parts=[banks[bk]] if c else (), tick=(c == 1))
                if hh == 0:
                    P.op("act", lambda e, bk=bk, kc=kc, hh=hh: e.copy(Vv[:, kc, hh * 512:(hh + 1) * 512], pf(bk)), reads=[banks[bk]], parts=[Vb])
                else:
                    P.op("dve", lambda e, bk=bk, kc=kc, hh=hh: e.tensor_copy(Vv[:, kc, hh * 512:(hh + 1) * 512], pf(bk)), reads=[banks[bk]], parts=[Vb])
        P.free(wvb)
        rfb = P.alloc("rope_feat", 2 * S * 2)
        rf = P.view(rfb, BF16, "p (k t) -> p k t", k=2)
        P.dma("sp", P.view(rfb)[0:64, :], rope_feat[:, :], writes=[rfb], sembuf=rfb)
        yab = P.alloc("y_attnT", 8 * S * 2)
        yav = P.view(yab, BF16, "p (h t) -> p h t", t=S)
        ssa_b = P.alloc("ss_a", S * 4)
        ss_a = P.view(ssa_b, F32)
        hwb = [P.alloc("hw%d" % i, (4 * 256 + 2 * 128) * 2) for i in range(2)]
        qnb = [P.alloc("qn%d" % i, S * 2) for i in range(2)]
        qpb = [P.alloc("qp%d" % i, S * 2) for i in range(2)]
        knb = [P.alloc("kn%d" % i, S * 2) for i in range(2)]
        ptb = [P.alloc("pT%d" % i, 512 * 2) for i in range(6)]
        densb = P.alloc("den_sb", 512 * 4)
        den_sb = P.view(densb, F32)
        r1b = P.alloc("r1", 512 * 4)
        r2b = P.alloc("r2", 512 * 4)
        sq2b = P.alloc("sq2", 512 * 2)
        rp1b = P.alloc("rp1", 512 * 4)
        rp2b = P.alloc("rp2", 512 * 4)
        r1 = P.view(r1b, F32)
        r2 = P.view(r2b, F32)
        sq2 = P.view(sq2b)
        rp1 = P.view(rp1b, F32)
        rp2 = P.view(rp2b, F32)
        QSCALE = (128 + 64) ** -0.5
        gen_rr = [0]

        def gen_bank():
            return 7

        ssacc_b = P.alloc("ssacc", S * 4)
        ssacc = P.view(ssacc_b, F32)
        onesfb = P.alloc("ones_f", 128 * 4)
        ones_f = P.view(onesfb, F32)
        P.op("dve", lambda e: e.memset(ones_f, 1.0), writes=[onesfb])

        def gen_head(h):
            sl = h % 2
            hb_ = hwb[sl]
            hv = P.view(hb_)
            wq = hv[:, 0:1024].rearrange("p (c n) -> p c n", n=256)
            wk = hv[:, 1024:1280].rearrange("p (c n) -> p c n", n=128)
            q_r = w_uq.rearrange("(c p) n -> p c n", p=128)
            k_r = w_ukv.rearrange("(c p) n -> p c n", p=128)
            P.dma("pool", wq[:, :, 0:192], q_r[:, :, h * 192:(h + 1) * 192], parts=[hb_], sembuf=hb_)
            P.dma("pool", wq[:, :, 192:224], q_r[:, :, h * 192 + 160:h * 192 + 192], parts=[hb_], sembuf=hb_)
            P.dma("pool", wq[:, :, 224:256], q_r[:, :, h * 192 + 128:h * 192 + 160], parts=[hb_], sembuf=hb_)
            P.dma("pool", wk, k_r[:, :, h * 256:h * 256 + 128], parts=[hb_], sembuf=hb_)
            qn = P.view(qnb[sl])
            qp = P.view(qpb[sl])
            kn = P.view(knb[sl])
            for tg in range(4):
                ts_ = slice(tg * 512, (tg + 1) * 512)
                bk = gen_bank()
                for c in range(4):
                    P.op("pe", lambda e, bk=bk, c=c, ts_=ts_, wq=wq: e.matmul(pf(bk), lhsT=wq[:, c, 0:128], rhs=cq_nT[:, c, ts_], start=(c == 0), stop=(c == 3)),
                         reads=[hb_, cqb], writes=[banks[bk]] if c == 0 else (), parts=[banks[bk]] if c else (), tick=(c == 3))
                P.op("dve", lambda e, bk=bk, ts_=ts_, qn=qn: e.tensor_scalar(qn[:, ts_], pf(bk), QSCALE, None, op0=ALU.mult), reads=[banks[bk]], parts=[qnb[sl]])
                yield
                bk = gen_bank()
                for c in range(2):
                    P.op("pe", lambda e, bk=bk, c=c, ts_=ts_, wk=wk: e.matmul(pf(bk), lhsT=wk[:, c, :], rhs=ckv_nT[:, c, ts_], start=(c == 0), stop=(c == 1)),
                         reads=[hb_, ckvb], writes=[banks[bk]] if c == 0 else (), parts=[banks[bk]] if c else (), tick=(c == 1))
                P.op("dve", lambda e, bk=bk, ts_=ts_, kn=kn: e.tensor_copy(kn[:, ts_], pf(bk)), reads=[banks[bk]], parts=[knb[sl]])
                yield
                bk = gen_bank()
                for c in range(4):
                    P.op("pe", lambda e, bk=bk, c=c, ts_=ts_, wq=wq: e.matmul(pf(bk, 0, 512, 64), lhsT=wq[:, c, 128:192], rhs=cq_nT[:, c, ts_], start=(c == 0), stop=(c == 3)),
                         reads=[hb_, cqb], writes=[banks[bk]] if c == 0 else (), parts=[banks[bk]] if c else (), tick=(c == 3))
                P.op("dve", lambda e, bk=bk, ts_=ts_: e.tensor_tensor(rp1[0:64, :], pf(bk, 0, 512, 64), rf[0:64, 0, ts_], op=ALU.mult), reads=[banks[bk], rfb], writes=[rp1b])
                yield
                bk2 = gen_bank()
                for c in range(4):
                    P.op("pe", lambda e, bk2=bk2, c=c, ts_=ts_, wq=wq: e.matmul(pf(bk2, 0, 512, 64), lhsT=wq[:, c, 192:256], rhs=cq_nT[:, c, ts_], start=(c == 0), stop=(c == 3)),
                         reads=[hb_, cqb], writes=[banks[bk2]] if c == 0 else (), parts=[banks[bk2]] if c else (), tick=(c == 3))
                P.op("dve", lambda e, bk2=bk2, ts_=ts_: e.tensor_tensor(rp2[0:64, :], pf(bk2, 0, 512, 64), rf[0:64, 1, ts_], op=ALU.mult), reads=[banks[bk2], rfb], writes=[rp2b])
                P.op("dve", lambda e, ts_=ts_, qp=qp: e.tensor_tensor(qp[0:64, ts_], rp1[0:64, :], rp2[0:64, :], op=ALU.add), reads=[rp1b, rp2b], parts=[qpb[sl]])
                yield

        att_it = [0]

        def attend_head(h, side):
            sl = h % 2
            qn = P.view(qnb[sl])
            qp = P.view(qpb[sl])
            kn = P.view(knb[sl])
            for qg in range(4):
                qs = slice(qg * 512, (qg + 1) * 512)
                bO = 4 + att_it[0] % 2
                bD = 6
                att_it[0] += 1

                def score(kc):
                    bk = kc % 4
                    ks = slice(kc * 128, (kc + 1) * 128)
                    P.op("pe", lambda e, bk=bk, ks=ks: e.matmul(pf(bk), lhsT=kn[:, ks], rhs=qn[:, qs], start=True, stop=False),
                         reads=[knb[sl], qnb[sl]], writes=[banks[bk]], tick=False)
                    P.op("pe", lambda e, bk=bk, ks=ks: e.matmul(pf(bk), lhsT=k_peT[0:64, ks], rhs=qp[0:64, qs], start=False, stop=True),
                         reads=[kpeb, qpb[sl]], parts=[banks[bk]])
                    pt_b = ptb[kc % 6]
                    pt = P.view(pt_b)
                    P.op("act", lambda e, bk=bk, pt=pt: e.activation(pt, pf(bk), AF.Exp), reads=[banks[bk]], writes=[pt_b])

                def pv_(kc):
                    pt_b = ptb[kc % 6]
                    pt = P.view(pt_b)
                    P.op("pe", lambda e, kc=kc, pt=pt: e.matmul(pf(bO), lhsT=Vv[:, kc, h * 128:(h + 1) * 128], rhs=pt, start=(kc == 0), stop=(kc == 15)),
                         reads=[Vb, pt_b], writes=[banks[bO]] if kc == 0 else (), parts=[banks[bO]] if kc else (), tick=False)
                    P.op("pe", lambda e, kc=kc, pt=pt: e.matmul(pf(bD), lhsT=ones, rhs=pt, start=(kc == 0), stop=(kc == 15)),
                         reads=[onesb, pt_b], writes=[banks[bD]] if kc == 0 else (), parts=[banks[bD]] if kc else ())

                for kc in range(4):
                    score(kc)
                for kc in range(0, 16, 2):
                    P.prewait("pe", reads=[ptb[(kc + 1) % 6]])
                    pv_(kc)
                    pv_(kc + 1)
                    if kc + 4 < 16:
                        score(kc + 4)
                        score(kc + 5)
                    if side is not None and kc >= 8:
                        next(side, None)
                P.op("dve", lambda e: e.tensor_copy(den_sb, pf(bD)), reads=[banks[bD]], writes=[densb])
                P.op("dve", lambda e: e.reciprocal(r1, den_sb), reads=[densb], writes=[r1b])
                P.op("dve", lambda e: e.tensor_tensor(r2, pf(bO), r1, op=ALU.mult), reads=[banks[bO], r1b], writes=[r2b])
                P.op("dve", lambda e: e.tensor_tensor(sq2, r2, r2, op=ALU.mult), reads=[r2b], writes=[sq2b])
                P.op("dve", lambda e, qs=qs: e.tensor_scalar(yav[:, h, qs], r2, pkc("again", h, 1), None, op0=ALU.mult), reads=[r2b, cbuf], parts=[yab])
                if h == 0:
                    P.op("dve", lambda e, qs=qs: e.tensor_copy(ssacc[:, qs], sq2), reads=[sq2b], parts=[ssacc_b])
                else:
                    P.op("dve", lambda e, qs=qs: e.tensor_tensor(ssacc[:, qs], ssacc[:, qs], sq2, op=ALU.add), reads=[sq2b, ssacc_b], parts=[ssacc_b])

        for _ in gen_head(0):
            pass
        for h in range(NH):
            side = gen_head(h + 1) if h + 1 < NH else None
            attend_head(h, side)
            if side is not None:
                for _ in side:
                    pass
        for qg in range(4):
            P.op("pe", lambda e, qg=qg: e.matmul(pf(qg), lhsT=ones_f, rhs=ssacc[:, qg * 512:(qg + 1) * 512], start=True, stop=True),
                 reads=[onesfb, ssacc_b], writes=[banks[qg]])
            P.op("dve", lambda e, qg=qg: e.tensor_copy(ss_a[:, qg * 512:(qg + 1) * 512], pf(qg)), reads=[banks[qg]], parts=[ssa_b])
        for b in [ssacc_b, onesfb]:
            P.free(b)
        for b in hwb + qnb + qpb + knb + ptb + [densb, r1b, r2b, sq2b, rp1b, rp2b, rfb, Vb, cqb, ckvb, kpeb]:
            P.free(b)
        P.op("act", lambda e: e.activation(ss_a, ss_a, AF.Sqrt, bias=pk_eps, scale=1.0 / HW), reads=[ssa_b, epsb], writes=[ssa_b])
        P.op("dve", lambda e: e.reciprocal(ss_a, ss_a), reads=[ssa_b], writes=[ssa_b])
        for h in range(NH):
            P.op("dve", lambda e, h=h: e.tensor_tensor(yav[:, h, :], yav[:, h, :], ss_a, op=ALU.mult), reads=[ssa_b, yab], writes=[yab])
        P.free(ssa_b)
        dump("d_ya", yab, P.view(yab))
        checkpoint(5)

        junkb = P.alloc("junk", 2048 * 2)
        junk = P.view(junkb)
        gpost_b, gpost = load_gain("g_post")
        gffn_b, gffn = load_gain("g_ffn")
        wob = P.alloc("w_out", 16 * D * 2)
        wo = P.view(wob, BF16, "p (k n) -> p k n", n=D)
        for k4 in range(4):
            P.dma("pool", wo[:, k4 * 4:(k4 + 1) * 4, :], w_out[k4 * 512:(k4 + 1) * 512, :].rearrange("(k p) n -> p k n", p=128), parts=[wob], sembuf=wob)
        xts = [P.alloc("xt%d" % i, D * 4) for i in range(2)]
        hts = [P.alloc("ht%d" % i, D * 4) for i in range(2)]
        hnbs = [P.alloc("hnb%d" % i, D * 2) for i in range(2)]
        hsts = [P.alloc("hst%d" % i, 16 * 128 * 2) for i in range(2)]
        ychunks = [Buf("ychunk%d" % i) for i in range(16)]
        hn_half = [Buf("hnscr%d" % i) for i in range(2)]
        yh_views = [P.view(yhb[hf], BF16, "p (c t) -> p c t", t=S) for hf in range(2)]

        def mixT(kc, tok):
            if kc < 4:
                return yh_views[0][:, kc, tok], yhb[0]
            if kc < 8:
                return yh_views[1][:, kc - 4, tok], yhb[1]
            return yav[:, kc - 8, tok], yab

        def w_mm(i):
            tok = slice(i * 128, (i + 1) * 128)
            bs = (i % 2) * 4
            xt_b = xts[i % 2]
            P.dma("sp", P.view(xt_b, F32), x[tok, :], writes=[xt_b], sembuf=xt_b)
            for dg in range(4):
                for kc in range(16):
                    lt, lb_ = mixT(kc, tok)
                    P.op("pe", lambda e, dg=dg, kc=kc, lt=lt, bs=bs: e.matmul(pf(bs + dg), lhsT=lt, rhs=wo[:, kc, dg * 512:(dg + 1) * 512], start=(kc == 0), stop=(kc == 15)),
                         reads=[lb_, wob], writes=[banks[bs + dg]] if kc == 0 else (), parts=[banks[bs + dg]] if kc else (), tick=(kc == 15))

        def w_evac(i):
            tok = slice(i * 128, (i + 1) * 128)
            bs = (i % 2) * 4
            xt_b, ht_b, hn_b = xts[i % 2], hts[i % 2], hnbs[i % 2]
            xt = P.view(xt_b, F32)
            ht = P.view(ht_b, F32)
            hnv = P.view(hn_b)
            mixed = ps_f[:, bs * 512:bs * 512 + 2048]
            bset = banks[bs:bs + 4]
            s1 = ss_col()
            P.op("act", lambda e, s1=s1: e.activation(junk, mixed, AF.Square, accum_out=s1), reads=bset, parts=[ssb])
            rs = rstd_from_ss(s1, 1.0 / D, ssb)
            P.op("dve", lambda e, ht=ht: e.tensor_tensor(ht, mixed, gpost, op=ALU.mult), reads=bset + [gpost_b], writes=[ht_b])
            P.op("dve", lambda e, ht=ht, rs=rs, xt=xt: e.scalar_tensor_tensor(ht, ht, rs, xt, op0=ALU.mult, op1=ALU.add), reads=[smallb, xt_b, ht_b], writes=[ht_b])
            P.dma("sp", y[tok, :], ht, reads=[ht_b], writes=[ychunks[i]], sembuf=ht_b)
            s2 = ss_col()
            P.op("act", lambda e, s2=s2, ht=ht: e.activation(junk, ht, AF.Square, accum_out=s2), reads=[ht_b], parts=[ssb])
            rs2 = rstd_from_ss(s2, 1.0 / D, ssb)
            P.op("dve", lambda e, ht=ht, rs2=rs2, hnv=hnv: e.scalar_tensor_tensor(hnv, ht, rs2, gffn, op0=ALU.mult, op1=ALU.mult), reads=[ht_b, smallb, gffn_b], writes=[hn_b])

        def w_tr(i):
            tok = slice(i * 128, (i + 1) * 128)
            bs = (i % 2) * 4
            hn_b, hs_b = hnbs[i % 2], hsts[i % 2]
            hnv = P.view(hn_b)
            hst = P.view(hs_b, BF16, "p (c t) -> p c t", t=128)
            for hb in range(2):
                bk = bs + hb
                for c in range(8):
                    cc = hb * 8 + c
                    P.op("pe", lambda e, bk=bk, c=c, cc=cc, hnv=hnv: e.transpose(pb(bk, c * 128, 128), hnv[:, cc * 128:(cc + 1) * 128], ident),
                         reads=[hn_b, identb], parts=[banks[bk]] if c else (), writes=[banks[bk]] if c == 0 else (), tick=(c == 7))
                P.op("act", lambda e, bk=bk, hb=hb, hst=hst: e.copy(hst[:, hb * 8:(hb + 1) * 8, :], pb(bk).rearrange("p (c t) -> p c t", t=128)), reads=[banks[bk]], parts=[hs_b])
            P.dma("sp", hn_scr[:, :, tok], hst, reads=[hs_b], parts=[hn_half[i // 8]], sembuf=hs_b)

        w_mm(0)
        for i in range(16):
            w_evac(i)
            if i + 1 < 16:
                w_mm(i + 1)
            w_tr(i)
        for b in xts + hts + hnbs + hsts + [wob, yab, yhb[0], yhb[1], gpost_b, gffn_b]:
            P.free(b)

        T = 1024
        actb = P.alloc("act", NFF * T * 2)
        actv = P.view(actb, BF16, "p (j t) -> p j t", t=T)
        hnTb = P.alloc("hnT", 16 * (T + 2) * 2)
        hnT = P.view(hnTb, BF16, "p (c t) -> p c t", t=T + 2)
        fbb = hnTb
        fbv = P.view(hnTb)[:, 0:8 * D].rearrange("p (i d) -> p i d", d=D)
        gpffn_b, gpffn = load_gain("g_pffn")
        wgub = [P.alloc("wgu%d" % i, 2 * 16 * 128 * 2) for i in range(2)]
        wdb = [P.alloc("wd%d" % i, 4 * 512 * 2) for i in range(3)]
        cob2 = [P.alloc("co%d" % i, T * 4) for i in range(2)]
        htin = [P.alloc("htin%d" % i, D * 4) for i in range(3)]
        ssq_b = P.alloc("ssq", 8 * 4 * 4)
        ssq = P.view(ssq_b, F32)
        for tile in range(2):
            T0 = tile * T
            if tile == 0:
                P.op("dve", lambda e: e.memset(hnT[:, :, 0:1], 0.0), parts=[hnTb])
                P.dma("sp", hnT[:, 0:8, 1:T + 2], hn_scr[:, 0:8, 0:T + 1], reads=hn_half, parts=[hnTb], sembuf=hnTb)
                P.dma("pool", hnT[:, 8:16, 1:T + 2], hn_scr[:, 8:16, 0:T + 1], reads=hn_half, parts=[hnTb], sembuf=hnTb)
            else:
                P.op("dve", lambda e: e.memset(hnT[:, :, T + 1:T + 2], 0.0), parts=[hnTb])
                P.dma("sp", hnT[:, 0:8, 0:T + 1], hn_scr[:, 0:8, T0 - 1:S], reads=hn_half, parts=[hnTb], sembuf=hnTb)
                P.dma("pool", hnT[:, 8:16, 0:T + 1], hn_scr[:, 8:16, T0 - 1:S], reads=hn_half, parts=[hnTb], sembuf=hnTb)
            for j in range(NFF):
                wb_ = wgub[j % 2]
                wgu = P.view(wb_, BF16, "p (k c n) -> p k c n", k=2, n=128)
                up_r = w_up.rearrange("(c p) n -> p c n", p=128)
                P.dma("pool", wgu[:, 0, :, :], up_r[:, :, j * 128:(j + 1) * 128], parts=[wb_], sembuf=wb_)
                P.dma("pool", wgu[:, 1, :, :], up_r[:, :, DFF + j * 128:DFF + (j + 1) * 128], parts=[wb_], sembuf=wb_)
                gs = 0 if j % 2 == 0 else 3
                for (off, n_) in ((0, 512), (512, 512), (1024, 2)):
                    bk = gs + off // 512
                    for c in range(16):
                        P.op("pe", lambda e, bk=bk, c=c, off=off, n_=n_, wgu=wgu: e.matmul(pf(bk, 0, n_), lhsT=wgu[:, 0, c, :], rhs=hnT[:, c, off:off + n_], start=(c == 0), stop=(c == 15)),
                             reads=[wb_, hnTb], writes=[banks[bk]] if c == 0 else (), parts=[banks[bk]] if c else (), tick=(c == 15))
                for g in range(2):
                    bk = 6 + g
                    for c in range(16):
                        P.op("pe", lambda e, bk=bk, c=c, g=g, wgu=wgu: e.matmul(pf(bk), lhsT=wgu[:, 1, c, :], rhs=hnT[:, c, 1 + g * 512:1 + (g + 1) * 512], start=(c == 0), stop=(c == 15)),
                             reads=[wb_, hnTb], writes=[banks[bk]] if c == 0 else (), parts=[banks[bk]] if c else (), tick=(c == 15))
                raw = ps_f[:, gs * 512:gs * 512 + T + 2]
                gset = banks[gs:gs + 3]
                co_b = cob2[j % 2]
                cov = P.view(co_b, F32)
                w0 = pkc("fconv_w", 0 * NFF + j, 1)
                w1 = pkc("fconv_w", 1 * NFF + j, 1)
                w2 = pkc("fconv_w", 2 * NFF + j, 1)
                bb = pkc("fconv_b", j, 1)
                P.op("act", lambda e, cov=cov, raw=raw, bb=bb, w1=w1: e.activation(cov, raw[:, 1:T + 1], AF.Identity, bias=bb, scale=w1), reads=gset + [cbuf], writes=[co_b])
                P.op("dve", lambda e, cov=cov, raw=raw, w0=w0: e.scalar_tensor_tensor(cov, raw[:, 0:T], w0, cov, op0=ALU.mult, op1=ALU.add), reads=gset + [cbuf, co_b], writes=[co_b])
                P.op("dve", lambda e, cov=cov, raw=raw, w2=w2: e.scalar_tensor_tensor(cov, raw[:, 2:T + 2], w2, cov, op0=ALU.mult, op1=ALU.add), reads=gset + [cbuf, co_b], writes=[co_b])
                P.op("act", lambda e, cov=cov: e.activation(cov, cov, AF.Gelu_apprx_tanh), reads=[co_b], writes=[co_b])
                P.op("dve", lambda e, cov=cov, j=j: e.tensor_tensor(actv[:, j, :], cov, ps_f[:, 6 * 512:8 * 512], op=ALU.mult), reads=[co_b, banks[6], banks[7]], parts=[actb])
            wd_r = w_down.rearrange("(k p) n -> p k n", p=128)
            wi = 0
            for dg in range(4):
                for k4 in range(NFF // 4):
                    wb_ = wdb[wi % 3]
                    wi += 1
                    wdv = P.view(wb_, BF16, "p (k n) -> p k n", n=512)
                    P.dma("pool", wdv, wd_r[:, k4 * 4:(k4 + 1) * 4, dg * 512:(dg + 1) * 512], writes=[wb_], sembuf=wb_)
                    for kk in range(4):
                        kc = k4 * 4 + kk
                        for tc in range(8):
                            P.op("pe", lambda e, tc=tc, kc=kc, kk=kk, wdv=wdv: e.matmul(pf(tc), lhsT=actv[:, kc, tc * 128:(tc + 1) * 128], rhs=wdv[:, kk, :], start=(kc == 0), stop=(kc == NFF - 1)),
                                 reads=[actb, wb_], writes=[banks[tc]] if kc == 0 else (), parts=[banks[tc]] if kc else (), tick=(kc == NFF - 1 or (kk == 3 and tc == 7)))
                for tc in range(8):
                    col = ssq[:, tc * 4 + dg:tc * 4 + dg + 1]
                    P.op("act", lambda e, tc=tc, col=col: e.activation(junk[:, 0:512], pf(tc), AF.Square, accum_out=col), reads=[banks[tc]], writes=[junkb], parts=[ssq_b])
                    P.op("dve", lambda e, tc=tc, dg=dg: e.tensor_copy(fbv[:, tc, dg * 512:(dg + 1) * 512], pf(tc)), reads=[banks[tc]], parts=[fbb])
            for tc in range(8):
                gi = tile * 8 + tc
                tok = slice(gi * 128, (gi + 1) * 128)
                hb_ = htin[tc % 3]
                hv = P.view(hb_, F32)
                P.dma("sp", hv, y[tok, :], reads=[ychunks[gi]], writes=[hb_], sembuf=hb_)
                s1 = ss_col()
                P.op("dve", lambda e, s1=s1, tc=tc: e.tensor_reduce(s1, ssq[:, tc * 4:tc * 4 + 4], axis=mybir.AxisListType.X, op=ALU.add), reads=[ssq_b], parts=[ssb])
                rs = rstd_from_ss(s1, 1.0 / D, ssb)
                for hh in range(2):
                    tb_ = cob2[hh]
                    tv_ = P.view(tb_, F32)
                    cs_ = slice(hh * 1024, (hh + 1) * 1024)
                    P.op("dve", lambda e, tc=tc, rs=rs, tv_=tv_, cs_=cs_: e.scalar_tensor_tensor(tv_, fbv[:, tc, cs_], rs, gpffn[:, cs_], op0=ALU.mult, op1=ALU.mult), reads=[fbb, smallb, gpffn_b], writes=[tb_])
                    P.op("dve", lambda e, hv=hv, tv_=tv_, cs_=cs_: e.tensor_tensor(hv[:, cs_], hv[:, cs_], tv_, op=ALU.add), reads=[tb_, hb_], writes=[hb_])
                P.dma("pool", y[tok, :], hv, reads=[hb_], writes=[ychunks[gi]], sembuf=hb_)
        if not P.dead:
            P.wait_all("sp", ychunks + P.dumps)

        block = es.enter_context(nc.Block())

        @block.tensor
        def _(e):
            for f in P.streams["pe"]:
                f(e)

        @block.scalar
        def _(e):
            for f in P.streams["act"]:
                f(e)

        @block.vector
        def _(e):
            for f in P.streams["dve"]:
                f(e)

        @block.gpsimd
        def _(e):
            for f in P.streams["pool"]:
                f(e)

        @block.sync
        def _(e):
            for f in P.streams["sp"]:
                f(e)

        build_program.stats = dict(peak=P.peak, nsem=P.nsem, n={k: len(v) for k, v in P.streams.items()})
    return nc


_NC_CACHE = {}


def _pack_params(inp):
    def col(v, nch):
        return np.ascontiguousarray(np.asarray(v, np.float32).reshape(nch, 128).T)

    def bc(v):
        v = np.asarray(v, np.float32).reshape(1, -1)
        return np.ascontiguousarray(np.broadcast_to(v, (128, v.shape[1])))

    def pad64(v):
        o = np.zeros((128, 1), np.float32)
        o[:64, 0] = np.asarray(v, np.float32)
        return o

    hcw = np.asarray(inp["hyena_conv_w"][0], np.float32)
    fcw = np.asarray(inp["ffn_conv_w"][0], np.float32)
    parts = [
        np.concatenate([col(hcw[k], 24) for k in range(3)], axis=1), col(inp["hyena_conv_b"][0], 24),
        col(inp["hyena_out_gain"][0], 8), col(inp["attn_out_gain"][0], 8),
        np.concatenate([col(fcw[k], NFF) for k in range(3)], axis=1), col(inp["ffn_conv_b"][0], NFF),
        pad64(inp["filt_b1"][0]), pad64(inp["filt_freq1"][0]), pad64(inp["filt_b2"][0]), pad64(inp["filt_freq2"][0]),
    ]
    gains = {"g_pre": bc(inp["pre_mix_gain"][0]), "g_q": bc(inp["q_norm_gain"][0]), "g_kv": bc(inp["kv_norm_gain"][0]),
             "g_post": bc(inp["post_mix_gain"][0]), "g_ffn": bc(inp["pre_ffn_gain"][0]), "g_pffn": bc(inp["post_ffn_gain"][0])}
    return np.ascontiguousarray(np.concatenate(parts, axis=1).astype(np.float32)), gains


def kernel(**inputs):
    inp = {k: np.asarray(v) for k, v in inputs.items()}
    if "nc" not in _NC_CACHE:
        _NC_CACHE["nc"] = build_program()
    nc = _NC_CACHE["nc"]
    cst = host_constants()
    f32 = lambda a: np.ascontiguousarray(np.asarray(a, np.float32))
    pack_np, gains_np = _pack_params(inp)
    shared = {
        "w_in": f32(inp["w_in"][0]), "w_uq": f32(inp["w_uq"][0]), "w_ukv": f32(inp["w_ukv"][0]), "w_out": f32(inp["w_out"][0]),
        "w_up": f32(inp["w_up"][0]), "w_down": f32(inp["w_down"][0]),
        "filt_w1": f32(inp["filt_w1"][0]), "filt_w2": f32(inp["filt_w2"][0]), "filt_w3": f32(inp["filt_w3"][0]),
        "pack": pack_np, "hbias": f32(inp["hyena_bias"][0]).reshape(1, HW),
        "dft_fwd": cst["dft_fwd"], "dft_inv": cst["dft_inv"], "window": cst["window"], "zT": cst["zT"],
        "rope_tok": cst["rope_tok"], "rope_feat": cst["rope_feat"], "ident": cst["ident"],
    }
    shared.update(gains_np)
    xs = f32(inp["x"])
    in_maps = []
    for b in range(NCORES):
        m = dict(shared)
        m["x"] = np.ascontiguousarray(xs[b])
        in_maps.append(m)
    res = run_bass_kernel_spmd(nc, in_maps, core_ids=list(range(NCORES)))
    kernel.last = res
    out = np.stack([np.asarray(r["y"], np.float32) for r in res.results], axis=0)
    return out
```
